# Optimizing a Trainium2 kernel written in Bass

```python
import math
import jax, jax.numpy as jnp
from jax import lax
import numpy as np

D_MODEL = 4096
BATCH = 2
SEQ = 4096
DEPTH = 2

MIX_WIDTH = D_MODEL
SSM_WIDTH = D_MODEL // 2
RET_WIDTH = MIX_WIDTH - SSM_WIDTH
SSM_GROUP = 16
SSM_GROUPS = SSM_WIDTH // SSM_GROUP
SSM_STATE = 64
RET_HEAD_DIM = 256
RET_HEADS = RET_WIDTH // RET_HEAD_DIM
RET_CHUNK = 128
D_FF = 4 * D_MODEL
IN_WIDTH = SSM_WIDTH + 4 * RET_WIDTH
ROPE_BASE = 10000.0
EPS = 1e-6
DT_MIN = 1e-3
DT_MAX = 1e-1

kernel_name = "hymba_s5_retnet_adaln_trunk"


def rmsnorm(x, g):
    xf = x.astype(jnp.float32)
    y = xf * lax.rsqrt(jnp.mean(jnp.square(xf), axis=-1, keepdims=True) + EPS)
    return (y * g.astype(jnp.float32)).astype(x.dtype)


def rotate(t, positions):
    dh = t.shape[-1]
    inv_freq = ROPE_BASE ** (-jnp.arange(0, dh, 2, dtype=jnp.float32) / dh)
    ang = positions.astype(jnp.float32)[..., None] * inv_freq
    cos = jnp.cos(ang)[:, :, None, :]
    sin = jnp.sin(ang)[:, :, None, :]
    t1, t2 = jnp.split(t, 2, axis=-1)
    return jnp.concatenate([t1 * cos - t2 * sin, t1 * sin + t2 * cos], axis=-1)


def s5_branch(u, lam_re, lam_im, log_dt, b_re, b_im, c_re, c_im, d_skip, w_glu, b_glu, g_out):
    f32 = jnp.float32
    bsz, seq, _ = u.shape
    uf = u.astype(f32).reshape(bsz, seq, SSM_GROUPS, SSM_GROUP)
    lam = lax.complex(lam_re.astype(f32), lam_im.astype(f32))
    dt = jnp.exp(log_dt.astype(f32))[:, None]
    lam_bar = jnp.exp(lam * dt)
    b_mat = lax.complex(b_re.astype(f32), b_im.astype(f32))
    b_bar = ((lam_bar - 1.0) / lam)[..., None] * b_mat
    c_mat = lax.complex(c_re.astype(f32), c_im.astype(f32))
    bu = jnp.einsum('bsgh,gph->bsgp', uf, b_bar)
    a = jnp.broadcast_to(lam_bar, bu.shape)

    def combine(left, right):
        a_l, b_l = left
        a_r, b_r = right
        return a_r * a_l, a_r * b_l + b_r

    _, states = lax.associative_scan(combine, (a, bu), axis=1)
    y = jnp.einsum('bsgp,ghp->bsgh', states, c_mat).real \
        + d_skip.astype(f32).reshape(SSM_GROUPS, SSM_GROUP) * uf
    y = jax.nn.gelu(y.reshape(bsz, seq, SSM_WIDTH))
    y = y * jax.nn.sigmoid(y @ w_glu.astype(f32) + b_glu.astype(f32))
    return rmsnorm(y, g_out)


def retention_branch(q, k, v, g, positions, g_norm):
    f32 = jnp.float32
    bsz, seq, _ = q.shape
    H, dh = RET_HEADS, RET_HEAD_DIM
    q = rotate(q.astype(f32).reshape(bsz, seq, H, dh), positions)
    k = rotate(k.astype(f32).reshape(bsz, seq, H, dh), positions) * (dh ** -0.5)
    v = v.astype(f32).reshape(bsz, seq, H, dh)

    log_gamma = jnp.log1p(-jnp.exp2(-5.0 - jnp.arange(H, dtype=f32)))
    idx = jnp.arange(RET_CHUNK, dtype=f32)
    rel = idx[:, None] - idx[None, :]
    decay_mask = jnp.where(rel >= 0,
                           jnp.exp(log_gamma[:, None, None] * jnp.maximum(rel, 0.0)),
                           0.0)
    cross_decay = jnp.exp(log_gamma[:, None] * (idx + 1.0))
    state_decay = jnp.exp(log_gamma[:, None] * (RET_CHUNK - 1.0 - idx))
    chunk_decay = jnp.exp(log_gamma * RET_CHUNK)
    n_chunks = seq // RET_CHUNK

    def to_chunks(t):
        return t.reshape(bsz, n_chunks, RET_CHUNK, H, dh).transpose(1, 0, 3, 2, 4)

    def step(state, qkv):
        qc, kc, vc = qkv
        scores = jnp.einsum('bhnd,bhmd->bhnm', qc, kc) * decay_mask
        inner = jnp.einsum('bhnm,bhme->bhne', scores, vc)
        cross = jnp.einsum('bhnd,bhde->bhne', qc, state) * cross_decay[None, :, :, None]
        new_state = state * chunk_decay[None, :, None, None] + jnp.einsum(
            'bhmd,bhme->bhde', kc * state_decay[None, :, :, None], vc)
        return new_state, inner + cross

    state0 = jnp.zeros((bsz, H, dh, dh), f32)
    _, out = lax.scan(step, state0, (to_chunks(q), to_chunks(k), to_chunks(v)))
    out = out.transpose(1, 0, 3, 2, 4).reshape(bsz, seq, H, dh)
    mu = jnp.mean(out, axis=-1, keepdims=True)
    var = jnp.mean(jnp.square(out - mu), axis=-1, keepdims=True)
    out = ((out - mu) * lax.rsqrt(var + EPS)).reshape(bsz, seq, RET_WIDTH)
    out = out * g_norm.astype(f32)
    return out * jax.nn.silu(g.astype(f32))


def setup_inputs(seed: int = 0) -> dict:
    key = jax.random.key(seed)
    ks = jax.random.split(key, 24)
    f32 = jnp.float32
    L, D, G, P, Hc = DEPTH, D_MODEL, SSM_GROUPS, SSM_STATE, SSM_GROUP

    def nrm(k, shape, scale):
        return jax.random.normal(k, shape, f32) * scale

    x = nrm(ks[0], (BATCH, SEQ, D), 1.0)
    c = nrm(ks[1], (BATCH, D), 1.0)
    offsets = jax.random.randint(ks[2], (BATCH, 1), 0, 1024, dtype=jnp.int32)
    positions = offsets + jnp.arange(SEQ, dtype=jnp.int32)[None, :]

    w_ada = nrm(ks[3], (L, D, 6 * D), 0.5 * D ** -0.5)
    b_ada = nrm(ks[4], (L, 6 * D), 0.02)
    g_mix = 1.0 + nrm(ks[5], (L, D), 0.02)
    w_in = nrm(ks[6], (L, D, IN_WIDTH), D ** -0.5)
    n_idx = jnp.arange(P, dtype=f32)
    lam_re = -0.5 + nrm(ks[7], (L, G, P), 0.01)
    lam_im = math.pi * n_idx + nrm(ks[8], (L, G, P), 0.01)
    log_dt = jax.random.uniform(ks[9], (L, G), f32, math.log(DT_MIN), math.log(DT_MAX))
    b_re = nrm(ks[10], (L, G, P, Hc), (2.0 * Hc) ** -0.5)
    b_im = nrm(ks[11], (L, G, P, Hc), (2.0 * Hc) ** -0.5)
    c_re = nrm(ks[12], (L, G, Hc, P), (2.0 * P) ** -0.5)
    c_im = nrm(ks[13], (L, G, Hc, P), (2.0 * P) ** -0.5)
    d_skip = nrm(ks[14], (L, SSM_WIDTH), 1.0)
    w_glu = nrm(ks[15], (L, SSM_WIDTH, SSM_WIDTH), SSM_WIDTH ** -0.5)
    b_glu = nrm(ks[16], (L, SSM_WIDTH), 0.02)
    g_ssm_out = 1.0 + nrm(ks[17], (L, SSM_WIDTH), 0.02)
    g_ret_norm = 1.0 + nrm(ks[18], (L, RET_WIDTH), 0.02)
    w_out = nrm(ks[19], (L, MIX_WIDTH, D), MIX_WIDTH ** -0.5)
    g_mlp = 1.0 + nrm(ks[20], (L, D), 0.02)
    w_up = nrm(ks[21], (L, D, D_FF), D ** -0.5)
    w_down = nrm(ks[22], (L, D_FF, D), D_FF ** -0.5)
    g_final = 1.0 + nrm(ks[23], (D,), 0.02)
    return {"x": x, "c": c, "positions": positions, "w_ada": w_ada, "b_ada": b_ada,
            "g_mix": g_mix, "w_in": w_in, "lam_re": lam_re, "lam_im": lam_im,
            "log_dt": log_dt, "b_re": b_re, "b_im": b_im, "c_re": c_re, "c_im": c_im,
            "d_skip": d_skip, "w_glu": w_glu, "b_glu": b_glu, "g_ssm_out": g_ssm_out,
            "g_ret_norm": g_ret_norm, "w_out": w_out, "g_mlp": g_mlp, "w_up": w_up,
            "w_down": w_down, "g_final": g_final}


def reference(x, c, positions, w_ada, b_ada, g_mix, w_in, lam_re, lam_im, log_dt,
              b_re, b_im, c_re, c_im, d_skip, w_glu, b_glu, g_ssm_out, g_ret_norm,
              w_out, g_mlp, w_up, w_down, g_final):
    split_points = [SSM_WIDTH + i * RET_WIDTH for i in range(4)]
    c_act = jax.nn.silu(c)
    for i in range(DEPTH):
        mod = (c_act @ w_ada[i] + b_ada[i])[:, None, :]
        shift1, scale1, gate1, shift2, scale2, gate2 = jnp.split(mod, 6, axis=-1)

        h = rmsnorm(x, g_mix[i]) * (1.0 + scale1) + shift1
        proj = h @ w_in[i]
        u, q, k, v, g = jnp.split(proj, split_points, axis=-1)
        ssm_out = s5_branch(u, lam_re[i], lam_im[i], log_dt[i], b_re[i], b_im[i],
                            c_re[i], c_im[i], d_skip[i], w_glu[i], b_glu[i], g_ssm_out[i])
        ret_out = retention_branch(q, k, v, g, positions, g_ret_norm[i])
        mixed = jnp.concatenate([ssm_out.astype(ret_out.dtype), ret_out], axis=-1)
        mixed = mixed.astype(x.dtype) @ w_out[i]
        x = x + gate1 * mixed

        h = rmsnorm(x, g_mlp[i]) * (1.0 + scale2) + shift2
        x = x + gate2 * (jnp.square(jax.nn.relu(h @ w_up[i])) @ w_down[i])
    return rmsnorm(x, g_final)
```

```python
import contextlib
import math
import numpy as np
import concourse.bass as bass
import concourse.mybir as mybir
from concourse.bass_utils import run_bass_kernel_spmd

F32 = mybir.dt.float32
BF16 = mybir.dt.bfloat16
I32 = mybir.dt.int32
AF = mybir.ActivationFunctionType
ALU = mybir.AluOpType

D = 4096
SEQ = 4096
NB = 2
DEPTH = 2
DFF = 16384
INW = 10240
SSMW = 2048
TB = 512
EPS = 1e-6
NCORES = 8
PI = math.pi

ENGS = ["pe", "act", "dve", "pool", "sp"]


class Op:
    __slots__ = ("eng", "fn", "deps", "idx", "signal", "count", "is_dma", "dsem", "dval", "inc")

    def __init__(self, eng, fn, is_dma):
        self.eng = eng
        self.fn = fn
        self.deps = set()
        self.idx = -1
        self.signal = False
        self.count = 0
        self.is_dma = is_dma
        self.dsem = -1
        self.dval = 0
        self.inc = 16


def _ap(x, p):
    return x(p) if callable(x) else x


class _Both:
    def __init__(self, a, b):
        self.a, self.b = a, b

    def then_inc(self, sem, v):
        self.a.then_inc(sem, v)
        self.b.then_inc(sem, v)


def _k(key):
    return tuple(key) if isinstance(key, (tuple, list)) else (key,)


class Sched:
    def __init__(self, n_dma_sems=48):
        self.ops = {e: [] for e in ENGS}
        self.res = {}
        self.n_dma_sems = n_dma_sems
        self.dma_last = [None] * n_dma_sems
        self.dma_rr = 0
        self.pending_dma = []

    def _related(self, key):
        d = self.res.setdefault(key[0], {})
        out = []
        for k, st in d.items():
            n = min(len(k), len(key))
            if k[:n] == key[:n]:
                out.append(st)
        return out

    def add(self, eng, fn, reads=(), writes=(), dma=False, inc=16, extra_deps=()):
        op = Op(eng, fn, dma)
        op.inc = inc
        deps = set(extra_deps)
        reads = [_k(x) for x in reads]
        writes = [_k(x) for x in writes]
        for key in reads:
            for st in self._related(key):
                if st["w"] is not None:
                    deps.add(st["w"])
        for key in writes:
            for st in self._related(key):
                if st["w"] is not None:
                    deps.add(st["w"])
                deps.update(st["r"].values())
                deps.update(st["rd"])
        for key in reads:
            d = self.res.setdefault(key[0], {})
            st = d.get(key)
            if st is None:
                st = {"w": None, "r": {}, "rd": []}
                d[key] = st
            if dma:
                st["rd"].append(op)
            else:
                st["r"][eng] = op
        for key in writes:
            d = self.res.setdefault(key[0], {})
            for k in list(d.keys()):
                if len(k) > len(key) and k[: len(key)] == key:
                    del d[k]
            d[key] = {"w": op, "r": {}, "rd": []}
        if dma:
            j = self.dma_rr
            self.dma_rr = (self.dma_rr + 1) % self.n_dma_sems
            prev = self.dma_last[j]
            if prev is not None:
                deps.add(prev)
            op.dsem = j
            op.dval = (prev.dval if prev is not None else 0) + inc
            self.dma_last[j] = op
            self.pending_dma.append(op)
        deps.discard(op)
        if eng == "pe" and not dma:
            deps = {x for x in deps if x.is_dma or x.eng != "pe"}
        op.deps = deps
        op.idx = len(self.ops[eng])
        self.ops[eng].append(op)
        for x in deps:
            if not x.is_dma:
                x.signal = True
        return op

    def barrier(self):
        deps = set(self.pending_dma)
        for e in ENGS:
            for op in reversed(self.ops[e]):
                if op.fn is not None and not op.is_dma:
                    deps.add(op)
                    break
        self.pending_dma = []
        self.res = {}
        for e in ENGS:
            self.add(e, None, extra_deps=deps)

    def setup(self, nc, stack):
        self.esem = {e: stack.enter_context(nc.semaphore("s_" + e)) for e in ENGS}
        self.dsem = [stack.enter_context(nc.semaphore("d_%d" % j)) for j in range(self.n_dma_sems)]
        self.cbase = {e: 0 for e in ENGS}
        self.waited = {e: {} for e in ENGS}
        self.n_emitted = 0

    def flush(self, nc):
        self.barrier()
        esem, dsem = self.esem, self.dsem
        for e in ENGS:
            c = self.cbase[e]
            for op in self.ops[e]:
                if op.signal and not op.is_dma:
                    c += 1
                    op.count = c
            self.cbase[e] = c
        sched = self
        ops = self.ops
        self.ops = {e: [] for e in ENGS}
        self.dyn = {}

        def run(e, engobj):
            waited = sched.waited[e]
            for op in ops[e]:
                for x in op.deps:
                    if x.is_dma:
                        key, val, sem = ("d", x.dsem), x.dval, dsem[x.dsem]
                    else:
                        key, val, sem = ("e", x.eng), x.count, esem[x.eng]
                    if waited.get(key, 0) >= val:
                        continue
                    waited[key] = val
                    engobj.wait_ge(sem, val)
                if op.fn is None:
                    continue
                ins = op.fn(engobj)
                sched.n_emitted += 1
                if op.is_dma:
                    ins.then_inc(dsem[op.dsem], op.inc)
                elif op.signal:
                    ins.then_inc(esem[e], 1)

        with nc.Block() as block:
            block.tensor(lambda t: run("pe", t))
            block.scalar(lambda s: run("act", s))
            block.vector(lambda v: run("dve", v))
            block.gpsimd(lambda g: run("pool", g))
            block.sync(lambda s: run("sp", s))


class KB:
    def __init__(self, nc, outer):
        self.nc = nc
        self.S = Sched()
        self.outer = outer
        self.ps = [outer.enter_context(nc.psum_tensor("ps%d" % i, [128, 512], F32)) for i in range(8)]
        self.ident_f = outer.enter_context(nc.sbuf_tensor("ident_f", [128, 128], F32))
        self.ident_b = outer.enter_context(nc.sbuf_tensor("ident_b", [128, 128], BF16))
        self.ones_b = outer.enter_context(nc.sbuf_tensor("ones_b", [128, 128], BF16))
        self.slab_rr = 0
        self.psg = 0
        self.uid = 0
        self.S.setup(nc, outer)
        self.pair = False
        self.rp = None

    def get_rp(self, e):
        if self.rp is None:
            self.rp = e.alloc_register("parity")
            e.reg_alu(self.rp, e.to_reg(e.partition_id()), 2, ALU.mod)
        return self.rp

    def pdma(self, e, mk):
        if not self.pair:
            return e.dma_start(**mk(0))
        rp = self.get_rp(e)
        with e.If_eq(rp, 0):
            a = e.dma_start(**mk(0))
        with e.Else():
            b = e.dma_start(**mk(1))
        return _Both(a, b)

    def consts(self, ident_dram):
        S = self.S
        S.add("sp", lambda e: e.dma_start(out=self.ident_f[:], in_=ident_dram), writes=["ident_f"], dma=True)
        S.add("pool", lambda e: e.dma_start(out=self.ident_b[:], in_=ident_dram), writes=["ident_b"], dma=True)
        S.add("dve", lambda e: e.memset(self.ones_b[:], 1.0), writes=["ones_b"])

    @contextlib.contextmanager
    def phase(self):
        st = contextlib.ExitStack()
        nc = self.nc
        self.uid += 1
        uid = self.uid

        def sb(name, shape, dt):
            return st.enter_context(nc.sbuf_tensor("%s_%d" % (name, uid), shape, dt))

        with st:
            yield sb
            self.S.flush(nc)

    def load_slab(self, slabs, w2d, k0, nk, c0, ncols=512):
        S = self.S
        slot = self.slab_rr % len(slabs)
        self.slab_rr += 1
        t = slabs[slot]
        key = ("slab", id(slabs), slot)
        step = 8
        for kk in range(0, nk, step):
            n = min(step, nk - kk)
            src = w2d[(k0 + kk) * 128:(k0 + kk + n) * 128, c0:c0 + ncols].rearrange("(k p) c -> p k c", p=128)
            S.add("pool", lambda e, kk=kk, n=n, src=src: e.dma_start(out=t[:, kk:kk + n, 0:ncols], in_=src),
                  writes=[key + (kk,)], dma=True)
        return t, key

    def psum_group(self):
        g = self.psg
        self.psg ^= 1
        return [g * 4 + j for j in range(4)]

    def gemm_fm(self, slabs, act, act_key, nkt, w2d, col0, ncols, evac, kslab=16):
        S = self.S
        for cs in range(col0, col0 + ncols, 512):
            nc_ = min(512, col0 + ncols - cs)
            nj = nc_ // 128
            banks = self.psum_group()
            for ks in range(0, nkt, kslab):
                nk = min(kslab, nkt - ks)
                t, key = self.load_slab(slabs, w2d, ks, nk, cs, nc_)
                for j in range(nj):
                    for k in range(nk):
                        kk = ks + k
                        S.add("pe", lambda e, j=j, k=k, kk=kk, t=t, b=banks[j]: e.matmul(
                            self.ps[b][:], t[:, k, j * 128:(j + 1) * 128], act[:, kk, :],
                            start=(kk == 0), stop=(kk == nkt - 1)),
                            reads=[key + ((k // 8) * 8,), act_key + (kk,)], writes=[("ps", banks[j])])
            for j in range(nj):
                evac(j, cs + j * 128, banks[j])

    def gemm_tm(self, slabs, act, act_key, nkt, w2d, col0, ncols, evac, ntt=4, kslab=16):
        S = self.S
        for cs in range(col0, col0 + ncols, 512):
            banks = self.psum_group()
            for ks in range(0, nkt, kslab):
                nk = min(kslab, nkt - ks)
                t, key = self.load_slab(slabs, w2d, ks, nk, cs, 512)
                for i in range(ntt):
                    for k in range(nk):
                        kk = ks + k
                        S.add("pe", lambda e, i=i, k=k, kk=kk, t=t, b=banks[i]: e.matmul(
                            self.ps[b][:], act[:, kk, i * 128:(i + 1) * 128], t[:, k, :],
                            start=(kk == 0), stop=(kk == nkt - 1)),
                            reads=[key + ((k // 8) * 8,), act_key + (kk,)], writes=[("ps", banks[i])])
            for i in range(ntt):
                evac(i, cs, banks[i])

    def normT(self, sb, x_rows, hT, hkey, gs, shift, gkey, tag):
        S = self.S
        xt = [sb("nx%s%d" % (tag, i), [128, D], F32) for i in range(2)]
        junk = sb("njunk" + tag, [128, D], BF16)
        st = sb("nst" + tag, [128, 8], F32)
        for i in range(4):
            x = xt[i % 2]
            xk = ("nx" + tag, i % 2)
            sk = ("nst" + tag, i % 2)
            c = (i % 2) * 4
            S.add("sp", lambda e, x=x, i=i: e.dma_start(out=x[:], in_=x_rows(i)), writes=[xk], dma=True)
            S.add("act", lambda e, x=x, c=c: e.activation(out=junk[:], in_=x[:], func=AF.Square, accum_out=st[:, c:c + 1]),
                  reads=[xk], writes=["njunk" + tag, sk])
            S.add("dve", lambda e, c=c: e.tensor_scalar(out=st[:, c + 1:c + 2], in0=st[:, c:c + 1], scalar1=1.0 / D, scalar2=EPS,
                                                      op0=ALU.mult, op1=ALU.add), reads=[sk], writes=[sk])
            S.add("act", lambda e, c=c: e.activation(out=st[:, c + 2:c + 3], in_=st[:, c + 1:c + 2], func=AF.Sqrt), reads=[sk], writes=[sk])
            S.add("dve", lambda e, c=c: e.reciprocal(out=st[:, c + 3:c + 4], in_=st[:, c + 2:c + 3]), reads=[sk], writes=[sk])
            S.add("dve", lambda e, x=x, c=c: e.tensor_scalar(out=x[:], in0=x[:], scalar1=st[:, c + 3:c + 4], scalar2=None, op0=ALU.mult),
                  reads=[xk, sk], writes=[xk])
            for t0 in range(0, 32, 4):
                bank = self.psum_group()[0]
                for q in range(4):
                    t = t0 + q
                    S.add("pe", lambda e, x=x, t=t, q=q, bank=bank: e.matmul(
                        self.ps[bank][:, q * 128:(q + 1) * 128], x[:, t * 128:(t + 1) * 128], self.ident_f[:],
                        start=True, stop=True), reads=[xk, "ident_f"], writes=[("ps", bank)])
                for q in range(4):
                    t = t0 + q
                    S.add("act", lambda e, t=t, q=q, bank=bank, i=i: e.activation(
                        out=hT[:, t, i * 128:(i + 1) * 128], in_=self.ps[bank][:, q * 128:(q + 1) * 128],
                        func=AF.Identity, scale=gs[:, t:t + 1], bias=shift[:, t:t + 1]),
                        reads=[("ps", bank), gkey], writes=[hkey + (t,)])

    def sin_tab(self, out, okey, ang, akeys, off, tmp, tmpi, tkey):
        S = self.S
        y, f = tmp[:, 0, :], tmp[:, 1, :]
        S.add("dve", lambda e: e.tensor_scalar(out=y, in0=ang, scalar1=1.0 / (2 * PI), scalar2=(off + PI) / (2 * PI),
                                               op0=ALU.mult, op1=ALU.add), reads=akeys, writes=[(tkey, 0)])
        S.add("dve", lambda e: e.tensor_copy(out=tmpi[:], in_=y), reads=[(tkey, 0)], writes=[(tkey, "i")])
        S.add("dve", lambda e: e.tensor_copy(out=f, in_=tmpi[:]), reads=[(tkey, "i")], writes=[(tkey, 1)])
        S.add("dve", lambda e: e.tensor_tensor(out=y, in0=y, in1=f, op=ALU.subtract), reads=[(tkey, 0), (tkey, 1)], writes=[(tkey, 0)])
        S.add("dve", lambda e: e.scalar_tensor_tensor(out=f, in0=y, scalar=0.0, in1=y, op0=ALU.is_lt, op1=ALU.add),
              reads=[(tkey, 0)], writes=[(tkey, 1)])
        S.add("dve", lambda e: e.tensor_scalar(out=f, in0=f, scalar1=2 * PI, scalar2=-PI, op0=ALU.mult, op1=ALU.add),
              reads=[(tkey, 1)], writes=[(tkey, 1)])
        S.add("act", lambda e: e.activation(out=out, in_=f, func=AF.Sin), reads=[(tkey, 1)], writes=[okey])

    def load_fm(self, dst, dkey, vec1d, ntile, col=0):
        src = vec1d.rearrange("(t p) -> p t", p=128)
        self.S.add("sp", lambda e: e.dma_start(out=dst[:, col:col + ntile], in_=src, allow_slow_non_contiguous=True),
                   writes=[dkey], dma=True)

    def load_bc(self, dst, dkey, vec1d, n):
        src = vec1d.partition_broadcast(128)
        self.S.add("sp", lambda e: e.dma_start(out=dst[:, 0:n], in_=src), writes=[dkey], dma=True)


def phase_mod(kb, c_vec, w_ada, b_ada, modout, col0, ncols, pstride=0):
    S = kb.S
    with kb.phase() as sb:
        cT = sb("cT", [128, 32], F32)
        slabs = [sb("mslab%d" % i, [128, 32, 512], F32) for i in range(2)]
        brow = [sb("brow%d" % i, [1, 512], F32) for i in range(2)]
        orow = [sb("orow%d" % i, [1, 512], F32) for i in range(2)]
        kb.load_fm(cT, ("cT",), c_vec, 32)
        S.add("act", lambda e: e.activation(out=cT[:], in_=cT[:], func=AF.Silu), reads=["cT"], writes=["cT"])
        for gi, cs in enumerate(range(col0, col0 + ncols, 512)):
            sl = slabs[gi % 2]
            skey = ("mslab", gi % 2)
            br, orr = brow[gi % 2], orow[gi % 2]
            for kk in range(0, 32, 8):
                S.add("sp", lambda e, sl=sl, kk=kk, cs=cs: kb.pdma(e, lambda p: dict(
                    out=sl[:, kk:kk + 8, :],
                    in_=w_ada[kk * 128:(kk + 8) * 128, cs + p * pstride:cs + p * pstride + 512].rearrange("(k p) c -> p k c", p=128))),
                    writes=[skey + (kk,)], dma=True)
            bank = kb.psum_group()[0]
            for k in range(32):
                S.add("pe", lambda e, sl=sl, k=k, bank=bank: e.matmul(kb.ps[bank][0:1, :], cT[:, k:k + 1], sl[:, k, :],
                                                                     start=(k == 0), stop=(k == 31)),
                      reads=["cT", skey + ((k // 8) * 8,)], writes=[("ps", bank)])
            S.add("sp", lambda e, cs=cs, br=br: kb.pdma(e, lambda p: dict(
                out=br[:], in_=b_ada[cs + p * pstride:cs + p * pstride + 512].rearrange("(a c) -> a c", a=1))), writes=[("brow", gi % 2)], dma=True)
            S.add("dve", lambda e, bank=bank, br=br, orr=orr: e.tensor_tensor(out=orr[:], in0=kb.ps[bank][0:1, :], in1=br[:], op=ALU.add),
                  reads=[("ps", bank), ("brow", gi % 2)], writes=[("orow", gi % 2)])
            S.add("sp", lambda e, cs=cs, orr=orr: kb.pdma(e, lambda p: dict(
                out=modout[cs + p * pstride:cs + p * pstride + 512].rearrange("(a c) -> a c", a=1), in_=orr[:])),
                reads=[("orow", gi % 2)], writes=["modout"], dma=True)


def phase_p1(kb, nblk, x_rows, pos, mod, g_mix, w_in, uT, qT, kT, v, sg, iota_p):
    S = kb.S
    with kb.phase() as sb:
        slabs = [sb("slab%d" % i, [128, 16, 512], BF16) for i in range(4)]
        hT = sb("hT", [128, 32, 512], BF16)
        gs = sb("gs", [128, 32], F32)
        sh = sb("sh", [128, 32], F32)
        gm = sb("gm", [128, 32], F32)
        invf = sb("invf", [128, 1], F32)
        posi = sb("posi", [128, 512], I32)
        posf = sb("posf", [128, 512], F32)
        tabs = sb("tabs", [128, 4, 512], F32)
        rtmp = sb("rtmp", [128, 2, 512], F32)
        rtmpi = sb("rtmpi", [128, 512], I32)
        rt = [sb("rt%d" % i, [128, 4, 512], F32) for i in range(2)]
        stage = [sb("stage%d" % i, [128, 4, 512], BF16) for i in range(2)]
        kb.load_fm(sh, ("sh",), mod[0:D], 32)
        kb.load_fm(gs, ("gs",), mod[D:2 * D], 32)
        kb.load_fm(gm, ("gm",), g_mix, 32)
        S.add("dve", lambda e: e.scalar_tensor_tensor(out=gs[:], in0=gs[:], scalar=1.0, in1=gm[:], op0=ALU.add, op1=ALU.mult),
              reads=["gs", "gm"], writes=["gs"])
        S.add("sp", lambda e: e.dma_start(out=invf[:], in_=iota_p.rearrange("(p a) -> p a", a=1)), writes=["invf"], dma=True)
        S.add("act", lambda e: e.activation(out=invf[:], in_=invf[:], func=AF.Exp, scale=-math.log(10000.0) / 128.0),
              reads=["invf"], writes=["invf"])
        cnt = [0]
        for blk in range(nblk):
            t0 = blk * TB
            S.add("sp", lambda e, t0=t0: e.dma_start(out=posi[:], in_=pos[t0:t0 + TB].partition_broadcast(128)), writes=["posi"], dma=True)
            S.add("dve", lambda e: e.tensor_copy(out=posf[:], in_=posi[:]), reads=["posi"], writes=["posf"])
            S.add("dve", lambda e: e.tensor_scalar(out=posf[:], in0=posf[:], scalar1=invf[:, 0:1], scalar2=None, op0=ALU.mult),
                  reads=["posf", "invf"], writes=["posf"])
            for ti, off in ((0, PI / 2), (1, 0.0)):
                kb.sin_tab(tabs[:, ti, :], ("tabs", ti), posf[:], ["posf"], off, rtmp, rtmpi, "rtmp")
                S.add("dve", lambda e, ti=ti: e.tensor_scalar(out=tabs[:, ti + 2, :], in0=tabs[:, ti, :], scalar1=1.0 / 16.0, scalar2=None, op0=ALU.mult),
                      reads=[("tabs", ti)], writes=[("tabs", ti + 2)])
            _normT_cached(kb, sb, lambda i, blk=blk: x_rows(blk, i), hT, ("hT",), gs, sh, "p1")

            def evac_u(j, col, bank, t0=t0):
                sg_ = stage[cnt[0] % 2]
                skey = ("stage", cnt[0] % 2)
                S.add("act", lambda e, j=j, bank=bank, sg_=sg_: e.activation(out=sg_[:, j, :], in_=kb.ps[bank][:], func=AF.Copy),
                      reads=[("ps", bank)], writes=[skey + (j,)])
                if j == 3:
                    c0 = col - 384
                    S.add("sp", lambda e, sg_=sg_, c0=c0: kb.pdma(e, lambda p: dict(
                        out=_ap(uT, p)[c0:c0 + 512, t0:t0 + TB].rearrange("(j p) t -> p j t", p=128), in_=sg_[:])), reads=[skey], dma=True)
                    cnt[0] += 1

            def mk_evac_rot(dstT, cbase, tc, ts, t0=t0):
                prev = {}

                def evac(j, col, bank):
                    sg_ = stage[cnt[0] % 2]
                    skey = ("stage", cnt[0] % 2)
                    if j % 2 == 0:
                        prev["b"] = bank
                        return
                    a, b = prev["b"], bank
                    r = rt[(j // 2) % 2]
                    rk = ("rt", (j // 2) % 2)
                    A, Bp = kb.ps[a], kb.ps[b]
                    S.add("dve", lambda e: e.tensor_tensor(out=r[:, 0, :], in0=A[:], in1=tabs[:, tc, :], op=ALU.mult),
                          reads=[("ps", a), ("tabs", tc)], writes=[rk + (0,)])
                    S.add("dve", lambda e: e.tensor_tensor(out=r[:, 1, :], in0=Bp[:], in1=tabs[:, ts, :], op=ALU.mult),
                          reads=[("ps", b), ("tabs", ts)], writes=[rk + (1,)])
                    S.add("dve", lambda e: e.tensor_tensor(out=r[:, 2, :], in0=A[:], in1=tabs[:, ts, :], op=ALU.mult),
                          reads=[("ps", a), ("tabs", ts)], writes=[rk + (2,)])
                    S.add("dve", lambda e: e.tensor_tensor(out=r[:, 3, :], in0=Bp[:], in1=tabs[:, tc, :], op=ALU.mult),
                          reads=[("ps", b), ("tabs", tc)], writes=[rk + (3,)])
                    S.add("dve", lambda e: e.tensor_tensor(out=sg_[:, j - 1, :], in0=r[:, 0, :], in1=r[:, 1, :], op=ALU.subtract),
                          reads=[rk + (0,), rk + (1,)], writes=[skey + (j - 1,)])
                    S.add("dve", lambda e: e.tensor_tensor(out=sg_[:, j, :], in0=r[:, 2, :], in1=r[:, 3, :], op=ALU.add),
                          reads=[rk + (2,), rk + (3,)], writes=[skey + (j,)])
                    if j == 3:
                        c0 = col - 384 - cbase
                        S.add("sp", lambda e, c0=c0: kb.pdma(e, lambda p: dict(
                            out=_ap(dstT, p)[c0:c0 + 512, t0:t0 + TB].rearrange("(j p) t -> p j t", p=128), in_=sg_[:])), reads=[skey], dma=True)
                        cnt[0] += 1
                return evac

            kb.gemm_fm(slabs, hT, ("hT",), 32, w_in, 0, 2048, evac_u)
            kb.gemm_fm(slabs, hT, ("hT",), 32, w_in, 2048, 2048, mk_evac_rot(qT, 2048, 0, 1))
            kb.gemm_fm(slabs, hT, ("hT",), 32, w_in, 4096, 2048, mk_evac_rot(kT, 4096, 2, 3))

            def mk_evac_tm(dst2d, cbase, func, t0=t0):
                def evac(i, cs, bank):
                    sg_ = stage[cnt[0] % 2]
                    skey = ("stage", cnt[0] % 2)
                    S.add("act", lambda e: e.activation(out=sg_[:, i, :], in_=kb.ps[bank][:], func=func),
                          reads=[("ps", bank)], writes=[skey + (i,)])
                    if i == 3:
                        c0 = cs - cbase
                        S.add("sp", lambda e, c0=c0: kb.pdma(e, lambda p: dict(
                            out=_ap(dst2d, p)[t0:t0 + TB, c0:c0 + 512].rearrange("(i p) c -> p i c", p=128), in_=sg_[:])), reads=[skey], dma=True)
                        cnt[0] += 1
                return evac

            kb.gemm_tm(slabs, hT, ("hT",), 32, w_in, 6144, 2048, mk_evac_tm(v, 6144, AF.Copy))
            kb.gemm_tm(slabs, hT, ("hT",), 32, w_in, 8192, 2048, mk_evac_tm(sg, 8192, AF.Silu))


def _normT_cached(kb, sb, x_rows, hT, hkey, gs, shift, tag):
    cache = kb.__dict__.setdefault("_ntc", {})
    key = (kb.uid, tag)
    if key not in cache:
        cache[key] = {
            "xt": [sb("nx%s%d" % (tag, i), [128, D], F32) for i in range(2)],
            "junk": sb("njunk" + tag, [128, D], BF16),
            "st": sb("nst" + tag, [128, 8], F32),
        }
    c = cache[key]

    def fake_sb(name, shape, dt):
        if name.startswith("nx"):
            return c["xt"][int(name[-1])]
        if name.startswith("njunk"):
            return c["junk"]
        return c["st"]

    kb.normT(fake_sb, x_rows, hT, hkey, gs, shift, ("gs",), tag)


S5X = "dve"


def phase_s5(kb, NUT, NBL, lam_re, lam_im, log_dt, b_re, b_im, c_re, c_im, d_skip, uT_in, ysT_out, iota512):
    S = kb.S
    NJ = 4 * NUT
    NCH = SEQ // TB
    with kb.phase() as sb:
        sc = sb("s5sc", [128, 16, NJ], F32)
        dsk = sb("s5dsk", [128, NUT], F32)
        io = sb("s5io", [128, 512], F32)
        stmp = sb("s5stmp", [128, 2, 512], F32)
        stmpi = sb("s5stmpi", [128, 512], I32)
        nat = [sb("s5nat%d" % i, [128, 128], F32) for i in range(4)]
        mats = sb("s5mats", [128, 4, 4, 128], BF16)
        tab = sb("s5tab", [128, 4, 5, 512], F32)
        ang = sb("s5ang", [128, 512], F32)
        ut_sb = [sb("s5u%d" % i, [128, SEQ], BF16) for i in range(2)]
        ys_sb = [sb("s5ys%d" % i, [128, SEQ], BF16) for i in range(2)]
        xb = sb("s5xb", [128, 4, 2, SEQ // 2], BF16)
        carry = sb("s5carry", [128, 4, 2], F32)
        xf = sb("s5xf", [128, 2, 2, 512], F32)
        wt = sb("s5wt", [128, 2, 6, 512], F32)
        ytmp = sb("s5ytmp", [128, 2, 512], F32)
        LRE, LIM, LDT, DT, AR, TH, R_, C1, S1, NRE, NIM, DEN, FRE, FIM, T1, T2 = range(16)

        def col(i):
            return sc[:, i, :]

        def dve(fn, r, w):
            S.add("dve", fn, reads=[("s5sc", x) for x in r], writes=[("s5sc", x) for x in w])

        S.add("sp", lambda e: e.dma_start(out=col(LRE), in_=lam_re.rearrange("(j a) p -> (a p) j", a=2), allow_slow_non_contiguous=True),
              writes=[("s5sc", LRE)], dma=True)
        S.add("sp", lambda e: e.dma_start(out=col(LIM), in_=lam_im.rearrange("(j a) p -> (a p) j", a=2), allow_slow_non_contiguous=True),
              writes=[("s5sc", LIM)], dma=True)
        for a in range(2):
            src = log_dt.rearrange("(j a) -> a j", a=2)[a].partition_broadcast(64)
            S.add("sp", lambda e, a=a, src=src: e.dma_start(out=sc[a * 64:(a + 1) * 64, LDT, :], in_=src, allow_slow_non_contiguous=True),
                  writes=[("s5sc", LDT, a)], dma=True)
        kb.load_fm(dsk, ("s5dsk",), d_skip, NUT)
        S.add("sp", lambda e: e.dma_start(out=io[:], in_=iota512.partition_broadcast(128)), writes=["s5io"], dma=True)
        S.add("act", lambda e: e.activation(out=col(DT), in_=col(LDT), func=AF.Exp), reads=[("s5sc", LDT)], writes=[("s5sc", DT)])
        dve(lambda e: e.tensor_tensor(out=col(AR), in0=col(LRE), in1=col(DT), op=ALU.mult), [LRE, DT], [AR])
        dve(lambda e: e.tensor_tensor(out=col(TH), in0=col(LIM), in1=col(DT), op=ALU.mult), [LIM, DT], [TH])
        S.add("act", lambda e: e.activation(out=col(R_), in_=col(AR), func=AF.Exp), reads=[("s5sc", AR)], writes=[("s5sc", R_)])
        kb.sin_tab(col(C1), ("s5sc", C1), col(TH), [("s5sc", TH)], PI / 2, stmp[:, :, 0:NJ], stmpi[:, 0:NJ], "s5stmp")
        kb.sin_tab(col(S1), ("s5sc", S1), col(TH), [("s5sc", TH)], 0.0, stmp[:, :, 0:NJ], stmpi[:, 0:NJ], "s5stmp")
        dve(lambda e: e.tensor_tensor(out=col(NRE), in0=col(R_), in1=col(C1), op=ALU.mult), [R_, C1], [NRE])
        dve(lambda e: e.tensor_scalar(out=col(NRE), in0=col(NRE), scalar1=-1.0, scalar2=None, op0=ALU.add), [NRE], [NRE])
        dve(lambda e: e.tensor_tensor(out=col(NIM), in0=col(R_), in1=col(S1), op=ALU.mult), [R_, S1], [NIM])
        dve(lambda e: e.tensor_tensor(out=col(DEN), in0=col(LRE), in1=col(LRE), op=ALU.mult), [LRE], [DEN])
        dve(lambda e: e.tensor_tensor(out=col(T1), in0=col(LIM), in1=col(LIM), op=ALU.mult), [LIM], [T1])
        dve(lambda e: e.tensor_tensor(out=col(DEN), in0=col(DEN), in1=col(T1), op=ALU.add), [DEN, T1], [DEN])
        dve(lambda e: e.reciprocal(out=col(DEN), in_=col(DEN)), [DEN], [DEN])
        dve(lambda e: e.tensor_tensor(out=col(T1), in0=col(NRE), in1=col(LRE), op=ALU.mult), [NRE, LRE], [T1])
        dve(lambda e: e.tensor_tensor(out=col(T2), in0=col(NIM), in1=col(LIM), op=ALU.mult), [NIM, LIM], [T2])
        dve(lambda e: e.tensor_tensor(out=col(T1), in0=col(T1), in1=col(T2), op=ALU.add), [T1, T2], [T1])
        dve(lambda e: e.tensor_tensor(out=col(FRE), in0=col(T1), in1=col(DEN), op=ALU.mult), [T1, DEN], [FRE])
        dve(lambda e: e.tensor_tensor(out=col(T1), in0=col(NIM), in1=col(LRE), op=ALU.mult), [NIM, LRE], [T1])
        dve(lambda e: e.tensor_tensor(out=col(T2), in0=col(NRE), in1=col(LIM), op=ALU.mult), [NRE, LIM], [T2])
        dve(lambda e: e.tensor_tensor(out=col(T1), in0=col(T1), in1=col(T2), op=ALU.subtract), [T1, T2], [T1])
        dve(lambda e: e.tensor_tensor(out=col(FIM), in0=col(T1), in1=col(DEN), op=ALU.mult), [T1, DEN], [FIM])

        seq_i = 0
        for ut in range(NUT):
            for jj in range(4):
                j = ut * 4 + jj
                srcs = [b_re, b_im, c_re, c_im]
                for mi in range(4):
                    n = nat[mi]
                    nk = ("s5nat", mi)
                    S.add("dve", lambda e, n=n: e.memset(n[:], 0.0), writes=[nk])
                    for a in range(2):
                        g = 2 * j + a
                        c0 = (jj * 2 + a) * 16
                        if mi < 2:
                            S.add("sp", lambda e, n=n, a=a, g=g, c0=c0, src=srcs[mi]: e.dma_start(out=n[a * 64:(a + 1) * 64, c0:c0 + 16], in_=src[g]),
                                  writes=[nk + (a,)], dma=True)
                        else:
                            S.add("sp", lambda e, n=n, a=a, g=g, c0=c0, src=srcs[mi]: e.dma_start(out=n[c0:c0 + 16, a * 64:(a + 1) * 64], in_=src[g]),
                                  writes=[nk + (a,)], dma=True)
                    bank = kb.psum_group()[0]
                    S.add("pe", lambda e, n=n, bank=bank: e.matmul(kb.ps[bank][:, 0:128], n[:], kb.ident_f[:], start=True, stop=True),
                          reads=[nk], writes=[("ps", bank)])
                    S.add("act", lambda e, bank=bank, jj=jj, mi=mi: e.activation(out=mats[:, jj, mi, :], in_=kb.ps[bank][:, 0:128], func=AF.Copy,
                                                                               scale=(-1.0 if mi == 3 else 1.0)),
                          reads=[("ps", bank)], writes=[("s5mats", jj, mi)])
                S.add("dve", lambda e, j=j: e.tensor_scalar(out=ang[:], in0=io[:], scalar1=sc[:, TH, j:j + 1], scalar2=None, op0=ALU.mult),
                      reads=["s5io", ("s5sc", TH)], writes=["s5ang"])
                kb.sin_tab(tab[:, jj, 2, :], ("s5tab", jj, 2), ang[:], ["s5ang"], PI / 2, stmp, stmpi, "s5stmp")
                kb.sin_tab(tab[:, jj, 3, :], ("s5tab", jj, 3), ang[:], ["s5ang"], 0.0, stmp, stmpi, "s5stmp")
                tk = ("s5tab", jj)
                S.add("dve", lambda e, jj=jj, j=j: e.tensor_scalar(out=tab[:, jj, 0, :], in0=tab[:, jj, 3, :], scalar1=sc[:, FIM, j:j + 1], scalar2=None, op0=ALU.mult),
                      reads=[tk + (3,), ("s5sc", FIM)], writes=[tk + (0,)])
                S.add("dve", lambda e, jj=jj, j=j: e.scalar_tensor_tensor(out=tab[:, jj, 0, :], in0=tab[:, jj, 2, :], scalar=sc[:, FRE, j:j + 1], in1=tab[:, jj, 0, :],
                                                                          op0=ALU.mult, op1=ALU.add), reads=[tk + (2,), tk + (0,), ("s5sc", FRE)], writes=[tk + (0,)])
                S.add("dve", lambda e, jj=jj, j=j: e.tensor_scalar(out=tab[:, jj, 1, :], in0=tab[:, jj, 3, :], scalar1=sc[:, FRE, j:j + 1], scalar2=None, op0=ALU.mult),
                      reads=[tk + (3,), ("s5sc", FRE)], writes=[tk + (1,)])
                S.add("dve", lambda e, jj=jj, j=j: e.scalar_tensor_tensor(out=tab[:, jj, 1, :], in0=tab[:, jj, 2, :], scalar=sc[:, FIM, j:j + 1], in1=tab[:, jj, 1, :],
                                                                          op0=ALU.mult, op1=ALU.subtract), reads=[tk + (2,), tk + (1,), ("s5sc", FIM)], writes=[tk + (1,)])
                S.add("dve", lambda e, jj=jj, j=j: e.tensor_scalar(out=tab[:, jj, 4, :], in0=io[:], scalar1=0.0, scalar2=sc[:, R_, j:j + 1], op0=ALU.mult, op1=ALU.add),
                      reads=["s5io", ("s5sc", R_)], writes=[tk + (4,)])
            for b in range(NBL):
                u = ut_sb[seq_i % 2]
                uk = ("s5u", seq_i % 2)
                ys = ys_sb[seq_i % 2]
                yk = ("s5ys", seq_i % 2)
                seq_i += 1
                S.add("sp", lambda e, u=u, ut=ut, b=b: kb.pdma(e, lambda p: dict(out=u[:], in_=_ap(uT_in, p)[ut * 128:(ut + 1) * 128, b, :])), writes=[uk], dma=True)
                for hf in range(2):
                    def chunk_ops(jj, c, par, hf=hf, u=u, uk=uk):
                        T = lambda i, jj=jj: tab[:, jj, i, :]
                        tk = ("s5tab", jj)
                        cs = slice(c * TB, (c + 1) * TB)
                        cl = slice((c % 4) * TB, (c % 4 + 1) * TB)
                        ba, bb = kb.psum_group()[0:2]
                        S.add("pe", lambda e, ba=ba, jj=jj, u=u, cs=cs: e.matmul(kb.ps[ba][:], mats[:, jj, 0, :], u[:, cs], start=True, stop=True),
                              reads=[("s5mats", jj, 0), uk], writes=[("ps", ba)])
                        yield
                        S.add("pe", lambda e, bb=bb, jj=jj, u=u, cs=cs: e.matmul(kb.ps[bb][:], mats[:, jj, 1, :], u[:, cs], start=True, stop=True),
                              reads=[("s5mats", jj, 1), uk], writes=[("ps", bb)])
                        yield
                        A, Bp = kb.ps[ba], kb.ps[bb]
                        W = lambda i, par=par: wt[:, par, i, :]
                        wk = ("s5wt", par)
                        S.add("dve", lambda e, A=A, W=W, T=T: e.tensor_tensor(out=W(0), in0=A[:], in1=T(0), op=ALU.mult), reads=[("ps", ba), tk + (0,)], writes=[wk + (0,)])
                        yield
                        S.add("dve", lambda e, Bp=Bp, W=W, T=T: e.tensor_tensor(out=W(1), in0=Bp[:], in1=T(1), op=ALU.mult), reads=[("ps", bb), tk + (1,)], writes=[wk + (1,)])
                        yield
                        S.add("dve", lambda e, W=W: e.tensor_tensor(out=W(4), in0=W(0), in1=W(1), op=ALU.subtract), reads=[wk + (0,), wk + (1,)], writes=[wk + (4,)])
                        yield
                        S.add("dve", lambda e, A=A, W=W, T=T: e.tensor_tensor(out=W(2), in0=A[:], in1=T(1), op=ALU.mult), reads=[("ps", ba), tk + (1,)], writes=[wk + (2,)])
                        yield
                        S.add("dve", lambda e, Bp=Bp, W=W, T=T: e.tensor_tensor(out=W(3), in0=Bp[:], in1=T(0), op=ALU.mult), reads=[("ps", bb), tk + (0,)], writes=[wk + (3,)])
                        yield
                        S.add("dve", lambda e, W=W: e.tensor_tensor(out=W(5), in0=W(2), in1=W(3), op=ALU.add), reads=[wk + (2,), wk + (3,)], writes=[wk + (5,)])
                        yield
                        xfk = ("s5xf", par)
                        pk = ("s5xf", 1 - par)
                        for ri in range(2):
                            init = 0.0 if c == 0 else carry[:, jj, ri:ri + 1]
                            S.add("dve", lambda e, W=W, T=T, ri=ri, init=init: e.tensor_tensor_scan(out=W(ri), data0=T(4), data1=W(4 + ri), initial=init,
                                                                                                   op0=ALU.mult, op1=ALU.add),
                                  reads=[tk + (4,), wk + (4 + ri,)] + ([] if c == 0 else [("s5carry", jj, ri)]), writes=[wk + (ri,)])
                            yield
                        S.add(S5X, lambda e, W=W, T=T: e.tensor_tensor(out=W(2), in0=W(0), in1=T(2), op=ALU.mult), reads=[wk + (0,), tk + (2,)], writes=[wk + (2,)])
                        yield
                        S.add(S5X, lambda e, W=W, T=T: e.tensor_tensor(out=W(3), in0=W(1), in1=T(3), op=ALU.mult), reads=[wk + (1,), tk + (3,)], writes=[wk + (3,)])
                        yield
                        S.add(S5X, lambda e, W=W, par=par: e.tensor_tensor(out=xf[:, par, 0, :], in0=W(2), in1=W(3), op=ALU.subtract),
                              reads=[wk + (2,), wk + (3,)], writes=[xfk + (0,)])
                        yield
                        S.add(S5X, lambda e, W=W, T=T: e.tensor_tensor(out=W(4), in0=W(0), in1=T(3), op=ALU.mult), reads=[wk + (0,), tk + (3,)], writes=[wk + (4,)])
                        yield
                        S.add(S5X, lambda e, W=W, T=T: e.tensor_tensor(out=W(5), in0=W(1), in1=T(2), op=ALU.mult), reads=[wk + (1,), tk + (2,)], writes=[wk + (5,)])
                        yield
                        S.add(S5X, lambda e, W=W, par=par: e.tensor_tensor(out=xf[:, par, 1, :], in0=W(4), in1=W(5), op=ALU.add),
                              reads=[wk + (4,), wk + (5,)], writes=[xfk + (1,)])
                        yield
                        for ri in range(2):
                            S.add("act", lambda e, par=par, ri=ri, jj=jj, cl=cl: e.activation(out=xb[:, jj, ri, cl], in_=xf[:, par, ri, :], func=AF.Copy),
                                  reads=[xfk + (ri,)], writes=[("s5xb", jj, ri, c % 4)])
                            yield
                            S.add("act", lambda e, par=par, ri=ri, jj=jj: e.activation(out=carry[:, jj, ri:ri + 1], in_=xf[:, par, ri, TB - 1:TB], func=AF.Copy),
                                  reads=[xfk + (ri,)], writes=[("s5carry", jj, ri)])
                            yield
                    for jp in range(2):
                        for c in range(hf * 4, hf * 4 + 4):
                            gens = [chunk_ops(2 * jp + si, c, si) for si in range(2)]
                            while gens:
                                for g_ in list(gens):
                                    try:
                                        next(g_)
                                    except StopIteration:
                                        gens.remove(g_)
                    for c in range(hf * 4, hf * 4 + 4):
                        cs = slice(c * TB, (c + 1) * TB)
                        cl = slice((c % 4) * TB, (c % 4 + 1) * TB)
                        bank = kb.psum_group()[0]
                        n = 0
                        for jj in range(4):
                            for ri in range(2):
                                S.add("pe", lambda e, bank=bank, jj=jj, ri=ri, cl=cl, n=n: e.matmul(kb.ps[bank][:], mats[:, jj, 2 + ri, :], xb[:, jj, ri, cl],
                                                                                                   start=(n == 0), stop=(n == 7)),
                                      reads=[("s5mats", jj, 2 + ri), ("s5xb", jj, ri, c % 4)], writes=[("ps", bank)])
                                n += 1
                        par = c % 2
                        S.add("dve", lambda e, u=u, cs=cs, ut=ut, bank=bank, par=par: e.scalar_tensor_tensor(
                            out=ytmp[:, par, :], in0=u[:, cs], scalar=dsk[:, ut:ut + 1], in1=kb.ps[bank][:], op0=ALU.mult, op1=ALU.add),
                            reads=[uk, "s5dsk", ("ps", bank)], writes=[("s5ytmp", par)])
                        S.add("act", lambda e, ys=ys, cs=cs, par=par: e.activation(out=ys[:, cs], in_=ytmp[:, par, :], func=AF.Gelu),
                              reads=[("s5ytmp", par)], writes=[yk + (c,)])
                S.add("sp", lambda e, ys=ys, ut=ut, b=b: kb.pdma(e, lambda p: dict(out=_ap(ysT_out, p)[ut * 128:(ut + 1) * 128, b, :], in_=ys[:])), reads=[yk], dma=True)


def phase_ret(kb, NH, NBL, hidx, g_ret_norm, qT_in, kT_in, v_in, sg_in, retT_out, iota512, iota_p):
    S = kb.S
    HS = SEQ // 2
    NCK = HS // 128
    with kb.phase() as sb:
        hv = sb("rhv", [128, 8, NH], F32)
        io = sb("rio", [128, 512], F32)
        ip = sb("rip", [128, 1], F32)
        dif = sb("rdif", [128, 128], F32)
        msk = sb("rmsk", [128, 128], F32)
        maskT = sb("rmaskT", [128, NH, 128], F32)
        cdT = sb("rcdT", [128, NH, 2, 128], F32)
        sdc = sb("rsdc", [128, NH], F32)
        ckd = sb("rckd", [128, NH], F32)
        gnT = sb("rgnT", [128, NH * 256], F32)
        qs = [sb("rq%d" % i, [128, 2, HS], BF16) for i in range(2)]
        ks = [sb("rk%d" % i, [128, 2, HS], BF16) for i in range(2)]
        vs = [sb("rv%d" % i, [128, NCK, 256], BF16) for i in range(2)]
        gsb = [sb("rg%d" % i, [128, NCK, 256], BF16) for i in range(2)]
        osb = [sb("ro%d" % i, [128, 2, HS], BF16) for i in range(2)]
        St = sb("rS", [128, 512], F32)
        Sb_ = sb("rSb", [128, 512], BF16)
        sm = [sb("rsm%d" % i, [128, 128], BF16) for i in range(2)]
        qd = [sb("rqd%d" % i, [128, 2, 128], BF16) for i in range(2)]
        kd = [sb("rkd%d" % i, [128, 256], BF16) for i in range(2)]
        on = [sb("ron%d" % i, [128, 256], F32) for i in range(2)]
        ob = [sb("rob%d" % i, [128, 256], BF16) for i in range(2)]
        stt = sb("rstt", [128, 2, 16], F32)

        S.add("sp", lambda e: e.dma_start(out=hv[:, 0, :], in_=hidx.partition_broadcast(128)), writes=[("rhv", 0)], dma=True)
        S.add("sp", lambda e: e.dma_start(out=io[:], in_=iota512.partition_broadcast(128)), writes=["rio"], dma=True)
        S.add("sp", lambda e: e.dma_start(out=ip[:], in_=iota_p.rearrange("(p a) -> p a", a=1)), writes=["rip"], dma=True)
        S.add("sp", lambda e: e.dma_start(out=gnT[:], in_=g_ret_norm.partition_broadcast(128)), writes=["rgnT"], dma=True)
        X, LG, T = 1, 2, 3
        S.add("dve", lambda e: e.tensor_scalar(out=hv[:, X, :], in0=hv[:, 0, :], scalar1=-math.log(2.0), scalar2=-5.0 * math.log(2.0), op0=ALU.mult, op1=ALU.add),
              reads=[("rhv", 0)], writes=[("rhv", X)])
        S.add("act", lambda e: e.activation(out=hv[:, X, :], in_=hv[:, X, :], func=AF.Exp), reads=[("rhv", X)], writes=[("rhv", X)])
        S.add("dve", lambda e: e.tensor_scalar(out=hv[:, T, :], in0=hv[:, X, :], scalar1=0.25, scalar2=1.0 / 3.0, op0=ALU.mult, op1=ALU.add),
              reads=[("rhv", X)], writes=[("rhv", T)])
        S.add("dve", lambda e: e.tensor_tensor(out=hv[:, T, :], in0=hv[:, T, :], in1=hv[:, X, :], op=ALU.mult), reads=[("rhv", X), ("rhv", T)], writes=[("rhv", T)])
        S.add("dve", lambda e: e.tensor_scalar(out=hv[:, T, :], in0=hv[:, T, :], scalar1=0.5, scalar2=None, op0=ALU.add), reads=[("rhv", T)], writes=[("rhv", T)])
        S.add("dve", lambda e: e.tensor_tensor(out=hv[:, T, :], in0=hv[:, T, :], in1=hv[:, X, :], op=ALU.mult), reads=[("rhv", X), ("rhv", T)], writes=[("rhv", T)])
        S.add("dve", lambda e: e.tensor_scalar(out=hv[:, T, :], in0=hv[:, T, :], scalar1=1.0, scalar2=None, op0=ALU.add), reads=[("rhv", T)], writes=[("rhv", T)])
        S.add("dve", lambda e: e.tensor_tensor(out=hv[:, LG, :], in0=hv[:, T, :], in1=hv[:, X, :], op=ALU.mult), reads=[("rhv", X), ("rhv", T)], writes=[("rhv", LG)])
        S.add("dve", lambda e: e.tensor_scalar(out=hv[:, LG, :], in0=hv[:, LG, :], scalar1=-1.0, scalar2=None, op0=ALU.mult), reads=[("rhv", LG)], writes=[("rhv", LG)])
        S.add("dve", lambda e: e.tensor_scalar(out=dif[:], in0=io[:, 0:128], scalar1=ip[:, 0:1], scalar2=-1.0, op0=ALU.subtract, op1=ALU.add),
              reads=["rio", "rip"], writes=["rdif"])
        S.add("dve", lambda e: e.tensor_scalar(out=msk[:], in0=dif[:], scalar1=0.0, scalar2=None, op0=ALU.is_ge), reads=["rdif"], writes=["rmsk"])
        S.add("dve", lambda e: e.tensor_scalar(out=dif[:], in0=dif[:], scalar1=0.0, scalar2=None, op0=ALU.max), reads=["rdif"], writes=["rdif"])
        S.add("act", lambda e: e.activation(out=ckd[:], in_=hv[:, LG, :], func=AF.Exp, scale=128.0), reads=[("rhv", LG)], writes=["rckd"])
        S.add("dve", lambda e: e.tensor_scalar(out=ip[:], in0=ip[:], scalar1=-1.0, scalar2=127.0, op0=ALU.mult, op1=ALU.add), reads=["rip"], writes=["rip"])
        S.add("dve", lambda e: e.tensor_scalar(out=sdc[:], in0=hv[:, LG, :], scalar1=ip[:, 0:1], scalar2=None, op0=ALU.mult), reads=["rip", ("rhv", LG)], writes=["rsdc"])
        S.add("act", lambda e: e.activation(out=sdc[:], in_=sdc[:], func=AF.Exp), reads=["rsdc"], writes=["rsdc"])
        for hh in range(NH):
            S.add("act", lambda e, hh=hh: e.activation(out=maskT[:, hh, :], in_=dif[:], func=AF.Exp, scale=hv[:, LG, hh:hh + 1]),
                  reads=["rdif", ("rhv", LG)], writes=[("rmaskT", hh)])
            S.add("dve", lambda e, hh=hh: e.tensor_tensor(out=maskT[:, hh, :], in0=maskT[:, hh, :], in1=msk[:], op=ALU.mult),
                  reads=[("rmaskT", hh), "rmsk"], writes=[("rmaskT", hh)])
            for dh in range(2):
                S.add("act", lambda e, hh=hh, dh=dh: e.activation(out=cdT[:, hh, dh, :], in_=io[:, 0:128], func=AF.Exp, scale=hv[:, LG, hh:hh + 1]),
                      reads=["rio", ("rhv", LG)], writes=[("rcdT", hh, dh)])

        li = 0
        for hh in range(NH):
            for b in range(NBL):
                S.add("dve", lambda e: e.memset(St[:], 0.0), writes=["rS"])
                S.add("dve", lambda e: e.memset(Sb_[:], 0.0), writes=["rSb"])
                for hf in range(2):
                    t0 = hf * HS
                    q, k, v, g, o = qs[li % 2], ks[li % 2], vs[li % 2], gsb[li % 2], osb[li % 2]
                    lk = li % 2
                    li += 1
                    qsrc = lambda p, hh=hh, b=b, t0=t0: _ap(qT_in, p)[hh * 256:(hh + 1) * 256, b, t0:t0 + HS].rearrange("(h p) t -> p h t", p=128)
                    ksrc = lambda p, hh=hh, b=b, t0=t0: _ap(kT_in, p)[hh * 256:(hh + 1) * 256, b, t0:t0 + HS].rearrange("(h p) t -> p h t", p=128)
                    vsrc = lambda p, hh=hh, b=b, t0=t0: _ap(v_in, p)[b, t0:t0 + HS, hh * 256:(hh + 1) * 256].rearrange("(c p) e -> p c e", p=128)
                    gsrc = lambda p, hh=hh, b=b, t0=t0: _ap(sg_in, p)[b, t0:t0 + HS, hh * 256:(hh + 1) * 256].rearrange("(c p) e -> p c e", p=128)
                    S.add("sp", lambda e, q=q, qsrc=qsrc: kb.pdma(e, lambda p: dict(out=q[:], in_=qsrc(p))), writes=[("rq", lk)], dma=True)
                    S.add("sp", lambda e, k=k, ksrc=ksrc: kb.pdma(e, lambda p: dict(out=k[:], in_=ksrc(p))), writes=[("rk", lk)], dma=True)
                    S.add("sp", lambda e, v=v, vsrc=vsrc: kb.pdma(e, lambda p: dict(out=v[:], in_=vsrc(p))), writes=[("rv", lk)], dma=True)
                    S.add("sp", lambda e, g=g, gsrc=gsrc: kb.pdma(e, lambda p: dict(out=g[:], in_=gsrc(p))), writes=[("rg", lk)], dma=True)
                    for c in range(NCK):
                        cs = slice(c * 128, (c + 1) * 128)
                        p = c % 2
                        bs, bo, bk = p, 2 + p, 4 + p
                        bkv, bt = 6, 7
                        for dh in range(2):
                            S.add("pe", lambda e, dh=dh, k=k, q=q, cs=cs, bs=bs: e.matmul(kb.ps[bs][:, 0:128], k[:, dh, cs], q[:, dh, cs], start=(dh == 0), stop=(dh == 1)),
                                  reads=[("rk", lk), ("rq", lk)], writes=[("ps", bs)])
                        S.add("dve", lambda e, p=p, bs=bs, hh=hh: e.tensor_tensor(out=sm[p][:], in0=kb.ps[bs][:, 0:128], in1=maskT[:, hh, :], op=ALU.mult),
                              reads=[("ps", bs), ("rmaskT", hh)], writes=[("rsm", p)])
                        S.add("dve", lambda e, p=p, q=q, cs=cs, hh=hh: e.tensor_tensor(out=qd[p][:], in0=q[:, :, cs], in1=cdT[:, hh, :, :], op=ALU.mult),
                              reads=[("rq", lk), ("rcdT", hh)], writes=[("rqd", p)])
                        S.add("pe", lambda e, p=p, v=v, c=c, bo=bo: e.matmul(kb.ps[bo][:, 0:256], sm[p][:], v[:, c, :], start=True, stop=False),
                              reads=[("rsm", p), ("rv", lk)], writes=[("ps", bo)])
                        for dh in range(2):
                            S.add("pe", lambda e, p=p, dh=dh, bo=bo: e.matmul(kb.ps[bo][:, 0:256], qd[p][:, dh, :], Sb_[:, dh * 256:(dh + 1) * 256], start=False, stop=(dh == 1)),
                                  reads=[("rqd", p), "rSb"], writes=[("ps", bo)])
                        for dh in range(2):
                            S.add("pe", lambda e, dh=dh, k=k, cs=cs, bk=bk: e.matmul(kb.ps[bk][:, dh * 128:(dh + 1) * 128], k[:, dh, cs], kb.ident_b[:], start=True, stop=True),
                                  reads=[("rk", lk)], writes=[("ps", bk)])
                        S.add("act", lambda e, p=p, bk=bk, hh=hh: e.activation(out=kd[p][:], in_=kb.ps[bk][:, 0:256], func=AF.Copy, scale=sdc[:, hh:hh + 1]),
                              reads=[("ps", bk), "rsdc"], writes=[("rkd", p)])
                        for dh in range(2):
                            S.add("pe", lambda e, p=p, dh=dh, v=v, c=c: e.matmul(kb.ps[bkv][:, dh * 256:(dh + 1) * 256], kd[p][:, dh * 128:(dh + 1) * 128], v[:, c, :], start=True, stop=True),
                                  reads=[("rkd", p), ("rv", lk)], writes=[("ps", bkv)])
                        S.add("dve", lambda e, hh=hh: e.scalar_tensor_tensor(out=St[:], in0=St[:], scalar=ckd[:, hh:hh + 1], in1=kb.ps[bkv][:], op0=ALU.mult, op1=ALU.add),
                              reads=["rS", "rckd", ("ps", bkv)], writes=["rS"])
                        S.add("act", lambda e: e.activation(out=Sb_[:], in_=St[:], func=AF.Copy), reads=["rS"], writes=["rSb"])
                        sk = ("rstt", p)
                        S.add("dve", lambda e, p=p, bo=bo: e.bn_stats(out=stt[:, p, 0:6], in_=kb.ps[bo][:, 0:256]), reads=[("ps", bo)], writes=[sk])
                        S.add("dve", lambda e, p=p: e.bn_aggr(out=stt[:, p, 6:8], in_=stt[:, p, 0:6]), reads=[sk], writes=[sk])
                        S.add("dve", lambda e, p=p: e.tensor_scalar(out=stt[:, p, 8:9], in0=stt[:, p, 7:8], scalar1=EPS, scalar2=None, op0=ALU.add), reads=[sk], writes=[sk])
                        S.add("act", lambda e, p=p: e.activation(out=stt[:, p, 9:10], in_=stt[:, p, 8:9], func=AF.Sqrt), reads=[sk], writes=[sk])
                        S.add("dve", lambda e, p=p: e.reciprocal(out=stt[:, p, 10:11], in_=stt[:, p, 9:10]), reads=[sk], writes=[sk])
                        S.add("dve", lambda e, p=p: e.scalar_tensor_tensor(out=stt[:, p, 11:12], in0=stt[:, p, 6:7], scalar=-1.0, in1=stt[:, p, 10:11], op0=ALU.mult, op1=ALU.mult),
                              reads=[sk], writes=[sk])
                        S.add("act", lambda e, p=p, bo=bo: e.activation(out=on[p][:], in_=kb.ps[bo][:, 0:256], func=AF.Identity, scale=stt[:, p, 10:11], bias=stt[:, p, 11:12]),
                              reads=[("ps", bo), sk], writes=[("ron", p)])
                        S.add("dve", lambda e, p=p, hh=hh: e.tensor_tensor(out=on[p][:], in0=on[p][:], in1=gnT[:, hh * 256:(hh + 1) * 256], op=ALU.mult),
                              reads=[("ron", p), "rgnT"], writes=[("ron", p)])
                        S.add("dve", lambda e, p=p, g=g, c=c: e.tensor_tensor(out=ob[p][:], in0=on[p][:], in1=g[:, c, :], op=ALU.mult),
                              reads=[("ron", p), ("rg", lk)], writes=[("rob", p)])
                        for eh in range(2):
                            S.add("pe", lambda e, p=p, eh=eh: e.matmul(kb.ps[bt][:, eh * 128:(eh + 1) * 128], ob[p][:, eh * 128:(eh + 1) * 128], kb.ident_b[:], start=True, stop=True),
                                  reads=[("rob", p)], writes=[("ps", bt)])
                        S.add("act", lambda e, o=o, cs=cs: e.activation(out=o[:, :, cs], in_=kb.ps[bt][:, 0:256].rearrange("p (h n) -> p h n", h=2), func=AF.Copy),
                              reads=[("ps", bt)], writes=[("ro", lk, c)])
                    S.add("sp", lambda e, o=o, hh=hh, b=b, t0=t0: kb.pdma(e, lambda p: dict(
                        out=_ap(retT_out, p)[hh * 256:(hh + 1) * 256, b, t0:t0 + HS].rearrange("(h p) t -> p h t", p=128), in_=o[:])), reads=[("ro", lk)], dma=True)


def phase_p3a(kb, nblk, x_rows, xmix_rows, mod, w_glu, b_glu, g_ssm_out, w_out, ysT, retT):
    S = kb.S
    with kb.phase() as sb:
        slabs = [sb("slab%d" % i, [128, 16, 512], BF16) for i in range(4)]
        mixT = sb("mixT", [128, 32, 512], BF16)
        ysb = sb("ysb", [128, 16, 512], BF16)
        y2 = sb("y2", [128, 16, 512], F32)
        sig = [sb("sig%d" % i, [128, 512], F32) for i in range(2)]
        sq = [sb("sq%d" % i, [128, 512], F32) for i in range(2)]
        ssacc = sb("ssacc", [128, 512], F32)
        ssb = sb("ssb", [128, 512], BF16)
        rbc = sb("rbc", [128, 512], F32)
        bg = sb("bg", [128, 16], F32)
        gso = sb("gso", [128, 16], F32)
        gate = sb("gate", [128, D], F32)
        xp = [sb("xp%d" % i, [128, 512], F32) for i in range(4)]
        xo = [sb("xo%d" % i, [128, 512], F32) for i in range(4)]
        kb.load_fm(bg, ("bg",), b_glu, 16)
        kb.load_fm(gso, ("gso",), g_ssm_out, 16)
        kb.load_bc(gate, ("gate",), mod[2 * D:3 * D], D)
        cnt = [0]
        for blk in range(nblk):
            t0 = blk * TB
            S.add("sp", lambda e, t0=t0: kb.pdma(e, lambda p: dict(out=ysb[:], in_=_ap(ysT, p)[:, t0:t0 + TB].rearrange("(t p) n -> p t n", p=128))), writes=["ysb"], dma=True)
            S.add("sp", lambda e, t0=t0: kb.pdma(e, lambda p: dict(out=mixT[:, 16:32, :], in_=_ap(retT, p)[:, t0:t0 + TB].rearrange("(t p) n -> p t n", p=128))),
                  writes=[("mixT", t) for t in range(16, 32)], dma=True)

            def evac_glu(j, col, bank):
                m = col // 128
                p = m % 2
                S.add("act", lambda e: e.activation(out=sig[p][:], in_=kb.ps[bank][:], func=AF.Sigmoid, bias=bg[:, m:m + 1]),
                      reads=[("ps", bank), "bg"], writes=[("sig", p)])
                S.add("dve", lambda e: e.tensor_tensor(out=y2[:, m, :], in0=ysb[:, m, :], in1=sig[p][:], op=ALU.mult),
                      reads=[("ysb", m), ("sig", p)], writes=[("y2", m)])
                S.add("act", lambda e: e.activation(out=sq[p][:], in_=y2[:, m, :], func=AF.Square), reads=[("y2", m)], writes=[("sq", p)])
                if m == 0:
                    S.add("dve", lambda e: e.tensor_copy(out=ssacc[:], in_=sq[p][:]), reads=[("sq", p)], writes=["ssacc"])
                else:
                    S.add("dve", lambda e: e.tensor_tensor(out=ssacc[:], in0=ssacc[:], in1=sq[p][:], op=ALU.add), reads=[("sq", p), "ssacc"], writes=["ssacc"])

            kb.gemm_fm(slabs, ysb, ("ysb",), 16, w_glu, 0, 2048, evac_glu)
            bank = kb.psum_group()[0]
            S.add("act", lambda e: e.activation(out=ssb[:], in_=ssacc[:], func=AF.Copy), reads=["ssacc"], writes=["ssb"])
            S.add("pe", lambda e, bank=bank: e.matmul(kb.ps[bank][:], kb.ones_b[:], ssb[:], start=True, stop=True), reads=["ssb", "ones_b"], writes=[("ps", bank)])
            S.add("dve", lambda e, bank=bank: e.tensor_scalar(out=rbc[:], in0=kb.ps[bank][:], scalar1=1.0 / SSMW, scalar2=EPS, op0=ALU.mult, op1=ALU.add),
                  reads=[("ps", bank)], writes=["rbc"])
            S.add("act", lambda e: e.activation(out=rbc[:], in_=rbc[:], func=AF.Sqrt), reads=["rbc"], writes=["rbc"])
            S.add("dve", lambda e: e.reciprocal(out=rbc[:], in_=rbc[:]), reads=["rbc"], writes=["rbc"])
            for m in range(16):
                S.add("dve", lambda e, m=m: e.scalar_tensor_tensor(out=mixT[:, m, :], in0=y2[:, m, :], scalar=gso[:, m:m + 1], in1=rbc[:], op0=ALU.mult, op1=ALU.mult),
                      reads=[("y2", m), "gso", "rbc"], writes=[("mixT", m)])

            def evac_out(i, cs, bank, blk=blk):
                p = cnt[0] % 4
                cnt[0] += 1
                S.add("sp", lambda e: e.dma_start(out=xp[p][:], in_=x_rows(blk, i)[:, cs:cs + 512]), writes=[("xp", p)], dma=True)
                S.add("dve", lambda e: e.tensor_tensor(out=xo[p][:], in0=kb.ps[bank][:], in1=gate[:, cs:cs + 512], op=ALU.mult),
                      reads=[("ps", bank), "gate"], writes=[("xo", p)])
                S.add("dve", lambda e: e.tensor_tensor(out=xo[p][:], in0=xo[p][:], in1=xp[p][:], op=ALU.add), reads=[("xo", p), ("xp", p)], writes=[("xo", p)])
                S.add("sp", lambda e: e.dma_start(out=xmix_rows(blk, i)[:, cs:cs + 512], in_=xo[p][:]), reads=[("xo", p)], dma=True)

            kb.gemm_tm(slabs, mixT, ("mixT",), 32, w_out, 0, D, evac_out)


def phase_p3b(kb, nblk, xmix_rows, xout_rows, mod, g_mlp, w_up, w_down):
    S = kb.S
    FC = 2048
    with kb.phase() as sbo:
        hT = sbo("hT2", [128, 32, 512], BF16)
        gs = sbo("gs2", [128, 32], F32)
        sh = sbo("sh2", [128, 32], F32)
        gm = sbo("gm2", [128, 32], F32)
        kb.load_fm(sh, ("sh2",), mod[3 * D:4 * D], 32)
        kb.load_fm(gs, ("gs2",), mod[4 * D:5 * D], 32)
        kb.load_fm(gm, ("gm2",), g_mlp, 32)
        S.add("dve", lambda e: e.scalar_tensor_tensor(out=gs[:], in0=gs[:], scalar=1.0, in1=gm[:], op0=ALU.add, op1=ALU.mult),
              reads=["gs2", "gm2"], writes=["gs2"])
        for blk in range(nblk):
            with kb.phase() as sb:
                _normT_cached(kb, sb, lambda i, blk=blk: xmix_rows(blk, i), hT, ("hT2",), gs, sh, "p3b")
            with kb.phase() as sb:
                slabs = [sb("slab%d" % i, [128, 16, 512], BF16) for i in range(3)]
                aC = sb("aC", [128, 16, 512], BF16)
                acc = [sb("acc%d" % i, [128, D], F32) for i in range(4)]
                rl = [sb("rl%d" % i, [128, 512], F32) for i in range(2)]
                gp = [sb("gp%d" % i, [128, 512], F32) for i in range(2)]
                xp = [sb("xp%d" % i, [128, 512], F32) for i in range(2)]
                for fc in range(DFF // FC):
                    def evac_up(j, col, bank, fc=fc):
                        m = (col - fc * FC) // 128
                        p = m % 2
                        S.add("act", lambda e: e.activation(out=rl[p][:], in_=kb.ps[bank][:], func=AF.Relu), reads=[("ps", bank)], writes=[("rl", p)])
                        S.add("dve", lambda e: e.tensor_tensor(out=aC[:, m, :], in0=rl[p][:], in1=rl[p][:], op=ALU.mult), reads=[("rl", p)], writes=[("aC", m)])

                    kb.gemm_fm(slabs, hT, ("hT2",), 32, w_up, fc * FC, FC, evac_up)

                    def evac_dn(i, cs, bank, fc=fc):
                        if fc == 0:
                            S.add("dve", lambda e: e.tensor_copy(out=acc[i][:, cs:cs + 512], in_=kb.ps[bank][:]), reads=[("ps", bank)], writes=[("acc", i, cs)])
                        else:
                            S.add("dve", lambda e: e.tensor_tensor(out=acc[i][:, cs:cs + 512], in0=acc[i][:, cs:cs + 512], in1=kb.ps[bank][:], op=ALU.add),
                                  reads=[("ps", bank), ("acc", i, cs)], writes=[("acc", i, cs)])

                    kb.gemm_tm(slabs, aC, ("aC",), 16, w_down[fc * FC:(fc + 1) * FC, :], 0, D, evac_dn)
                n = 0
                for cs in range(0, D, 512):
                    g = gp[(cs // 512) % 2]
                    gk = ("gp", (cs // 512) % 2)
                    S.add("sp", lambda e, g=g, cs=cs: e.dma_start(out=g[:], in_=mod[5 * D + cs:5 * D + cs + 512].partition_broadcast(128)), writes=[gk], dma=True)
                    for i in range(4):
                        p = n % 2
                        n += 1
                        S.add("sp", lambda e, p=p, i=i, cs=cs, blk=blk: e.dma_start(out=xp[p][:], in_=xmix_rows(blk, i)[:, cs:cs + 512]), writes=[("xp", p)], dma=True)
                        S.add("dve", lambda e, i=i, cs=cs, g=g: e.tensor_tensor(out=acc[i][:, cs:cs + 512], in0=acc[i][:, cs:cs + 512], in1=g[:], op=ALU.mult),
                              reads=[("acc", i, cs), gk], writes=[("acc", i, cs)])
                        S.add("dve", lambda e, i=i, cs=cs, p=p: e.tensor_tensor(out=acc[i][:, cs:cs + 512], in0=acc[i][:, cs:cs + 512], in1=xp[p][:], op=ALU.add),
                              reads=[("acc", i, cs), ("xp", p)], writes=[("acc", i, cs)])
                for i in range(4):
                    S.add("sp", lambda e, i=i, blk=blk: e.dma_start(out=xout_rows(blk, i), in_=acc[i][:]), reads=[("acc", i)], dma=True)


def phase_fn(kb, ntiles, x_rows, y_rows, g_final):
    S = kb.S
    with kb.phase() as sb:
        gb = sb("fgb", [128, D], F32)
        xt = [sb("fx%d" % i, [128, D], F32) for i in range(2)]
        junk = sb("fjunk", [128, D], BF16)
        st = sb("fst", [128, 8], F32)
        kb.load_bc(gb, ("fgb",), g_final, D)
        for i in range(ntiles):
            x = xt[i % 2]
            xk = ("fx", i % 2)
            sk = ("fst", i % 2)
            c = (i % 2) * 4
            S.add("sp", lambda e, x=x, i=i: e.dma_start(out=x[:], in_=x_rows(i)), writes=[xk], dma=True)
            S.add("act", lambda e, x=x, c=c: e.activation(out=junk[:], in_=x[:], func=AF.Square, accum_out=st[:, c:c + 1]), reads=[xk], writes=["fjunk", sk])
            S.add("dve", lambda e, c=c: e.tensor_scalar(out=st[:, c + 1:c + 2], in0=st[:, c:c + 1], scalar1=1.0 / D, scalar2=EPS, op0=ALU.mult, op1=ALU.add), reads=[sk], writes=[sk])
            S.add("act", lambda e, c=c: e.activation(out=st[:, c + 2:c + 3], in_=st[:, c + 1:c + 2], func=AF.Sqrt), reads=[sk], writes=[sk])
            S.add("dve", lambda e, c=c: e.reciprocal(out=st[:, c + 3:c + 4], in_=st[:, c + 2:c + 3]), reads=[sk], writes=[sk])
            S.add("dve", lambda e, x=x, c=c: e.scalar_tensor_tensor(out=x[:], in0=x[:], scalar=st[:, c + 3:c + 4], in1=gb[:], op0=ALU.mult, op1=ALU.mult),
                  reads=[xk, sk, "fgb"], writes=[xk])
            S.add("sp", lambda e, x=x, i=i: e.dma_start(out=y_rows(i), in_=x[:]), reads=[xk], dma=True)


_W_SPECS = [
    ("w_ada", [DEPTH, D, 6 * D]), ("b_ada", [DEPTH, 6 * D]), ("g_mix", [DEPTH, D]), ("w_in", [DEPTH, D, INW]),
    ("lam_re", [DEPTH, 128, 64]), ("lam_im", [DEPTH, 128, 64]), ("log_dt", [DEPTH, 128]),
    ("b_re", [DEPTH, 128, 64, 16]), ("b_im", [DEPTH, 128, 64, 16]), ("c_re", [DEPTH, 128, 16, 64]), ("c_im", [DEPTH, 128, 16, 64]),
    ("d_skip", [DEPTH, SSMW]), ("w_glu", [DEPTH, SSMW, SSMW]), ("b_glu", [DEPTH, SSMW]), ("g_ssm_out", [DEPTH, SSMW]),
    ("g_ret_norm", [DEPTH, SSMW]), ("w_out", [DEPTH, D, D]), ("g_mlp", [DEPTH, D]), ("w_up", [DEPTH, D, DFF]),
    ("w_down", [DEPTH, DFF, D]), ("g_final", [D]),
]


def build_fused(debug=False):
    nc = bass.Bass("TRN2", target_bir_lowering=False)

    def din(name, shape, dt=F32):
        return nc.dram_tensor(name, shape, dt, kind="ExternalInput").ap()

    def dtmp(name, shape, dt, out=False):
        if out:
            return nc.dram_tensor(name, shape, dt, kind="ExternalOutput").ap()
        return nc.dram_tensor(name, shape, dt).ap()

    x = din("x", [SEQ, D])
    c = din("c", [D])
    pos = din("pos", [SEQ], I32)
    W = {n: din(n, s) for n, s in _W_SPECS}
    ident = din("ident", [128, 128])
    iota = din("iota", [128])
    iota512 = din("iota512", [512])
    hidx = din("hidx", [8])
    y = nc.dram_tensor("y", [SEQ, D], F32, kind="ExternalOutput").ap()
    modbuf = dtmp("modbuf", [DEPTH, 6 * D], F32, out=debug)
    xa = dtmp("xa", [SEQ, D], F32)
    xb = dtmp("xb", [SEQ, D], F32)
    xdbg = dtmp("xdbg", [TB, D], F32, out=True) if debug else None
    uT = dtmp("uT", [SSMW, 1, SEQ], BF16)
    qT = dtmp("qT", [SSMW, 1, SEQ], BF16)
    kT = dtmp("kT", [SSMW, 1, SEQ], BF16)
    v = dtmp("v", [1, SEQ, SSMW], BF16)
    sg = dtmp("sg", [1, SEQ, SSMW], BF16)
    ysT = dtmp("ysT", [SSMW, 1, SEQ], BF16)
    retT = dtmp("retT", [SSMW, 1, SEQ], BF16)
    nblk = SEQ // TB

    def rows(t):
        return lambda blk, i: t[blk * TB + i * 128: blk * TB + (i + 1) * 128, :]

    with contextlib.ExitStack() as outer:
        kb = KB(nc, outer)
        kb.consts(ident)
        for l in range(DEPTH):
            phase_mod(kb, c, W["w_ada"][l], W["b_ada"][l], modbuf[l], 0, 6 * D)
        for l in range(DEPTH):
            xin = x if l == 0 else xb
            phase_p1(kb, nblk, rows(xin), pos, modbuf[l], W["g_mix"][l], W["w_in"][l],
                     uT[:, 0, :], qT[:, 0, :], kT[:, 0, :], v[0], sg[0], iota)
            phase_s5(kb, SSMW // 128, 1, W["lam_re"][l], W["lam_im"][l], W["log_dt"][l], W["b_re"][l], W["b_im"][l],
                     W["c_re"][l], W["c_im"][l], W["d_skip"][l], uT, ysT, iota512)
            phase_ret(kb, 8, 1, hidx, W["g_ret_norm"][l], qT, kT, v, sg, retT, iota512, iota)
            phase_p3a(kb, nblk, rows(xin), rows(xa), modbuf[l], W["w_glu"][l], W["b_glu"][l], W["g_ssm_out"][l], W["w_out"][l],
                      ysT[:, 0, :], retT[:, 0, :])
            phase_p3b(kb, nblk, rows(xa), rows(xb), modbuf[l], W["g_mlp"][l], W["w_up"][l], W["w_down"][l])
            if debug and l == 0:
                with kb.phase() as sb:
                    t = sb("dbgt", [128, D], F32)
                    for i in range(4):
                        kb.S.add("sp", lambda e, i=i: e.dma_start(out=t[:], in_=xb[i * 128:(i + 1) * 128, :]), writes=["dbgt"], dma=True)
                        kb.S.add("sp", lambda e, i=i: e.dma_start(out=xdbg[i * 128:(i + 1) * 128, :], in_=t[:]), reads=["dbgt"], dma=True)
        phase_fn(kb, SEQ // 128, lambda i: xb[i * 128:(i + 1) * 128, :], lambda i: y[i * 128:(i + 1) * 128, :], W["g_final"])
        n_emitted = kb.S.n_emitted
    return nc, n_emitted


def kernel2(**inputs):
    nc, _ = build_fused(debug=False)
    f32 = np.float32
    consts = {
        "ident": np.eye(128, dtype=f32),
        "iota": np.arange(128, dtype=f32),
        "iota512": np.arange(1, 513, dtype=f32),
        "hidx": np.arange(8, dtype=f32),
    }
    in_maps = []
    for b in range(NB):
        m = {"x": np.ascontiguousarray(inputs["x"][b], dtype=f32),
             "c": np.ascontiguousarray(inputs["c"][b], dtype=f32),
             "pos": np.ascontiguousarray(inputs["positions"][b], dtype=np.int32)}
        for n, s in _W_SPECS:
            m[n] = np.ascontiguousarray(inputs[n], dtype=f32)
        m.update(consts)
        in_maps.append(m)
    res = run_bass_kernel_spmd(nc, in_maps, core_ids=list(range(NB)))
    return np.stack([np.asarray(res.results[b]["y"]) for b in range(NB)], axis=0).astype(f32)


HALF = SEQ // 2


def pair_barrier(kb, flags, nonce, k):
    def fn(sp):
        rp = kb.get_rp(sp)
        with sp.register("bn%d" % k) as rn, sp.register("br%d" % k) as r, sp.register("bc%d" % k) as r2, sp.register("bv%d" % k) as rv:
            sp.load(rn, nonce[0:1, 0:1])
            sp.reg_alu(rn, rn, 16, ALU.mult)
            sp.reg_alu(rv, rn, k, ALU.add)
            for p in range(2):
                g = sp.If_eq(rp, 0) if p == 0 else sp.Else()
                with g:
                    sp.store(flags[p:p + 1, 0:1], rv)
                    sp.reg_mov(r2, 1)
                    with sp.While(r2):
                        sp.load(r, flags[1 - p:2 - p, 0:1])
                        sp.reg_alu(r, r, rn, ALU.subtract)
                        sp.reg_alu(r2, r, k, ALU.is_lt)
                        sp.reg_alu(r, r, 15, ALU.is_gt)
                        sp.reg_alu(r2, r2, r, ALU.add)
        return sp.nop()
    kb.S.add("sp", fn)


_LOC_SPECS = [
    ("lam_re_l", [DEPTH, 64, 64]), ("lam_im_l", [DEPTH, 64, 64]), ("log_dt_l", [DEPTH, 64]),
    ("b_re_l", [DEPTH, 64, 64, 16]), ("b_im_l", [DEPTH, 64, 64, 16]), ("c_re_l", [DEPTH, 64, 16, 64]), ("c_im_l", [DEPTH, 64, 16, 64]),
    ("d_skip_l", [DEPTH, 1024]), ("g_ret_norm_l", [DEPTH, 1024]),
]
_W4_SPECS = [(n, s) for n, s in _W_SPECS if n not in ("lam_re", "lam_im", "log_dt", "b_re", "b_im", "c_re", "c_im", "d_skip", "g_ret_norm")]


def build_fused4():
    nc = bass.Bass("TRN2", target_bir_lowering=False)

    def din(name, shape, dt=F32):
        return nc.dram_tensor(name, shape, dt, kind="ExternalInput").ap()

    x = din("x", [HALF, D])
    c = din("c", [D])
    pos = din("pos", [HALF], I32)
    W = {n: din(n, s) for n, s in _W4_SPECS}
    WL = {n: din(n, s) for n, s in _LOC_SPECS}
    ident = din("ident", [128, 128])
    iota = din("iota", [128])
    iota512 = din("iota512", [512])
    hidx = din("hidx", [4])
    nonce = din("nonce", [1, 16], I32)
    y = nc.dram_tensor("y", [HALF, D], F32, kind="ExternalOutput").ap()
    modbuf = nc.dram_tensor("modbuf", [DEPTH, 6 * D], F32, addr_space="Shared").ap()
    xa = nc.dram_tensor("xa", [HALF, D], F32).ap()
    xb = nc.dram_tensor("xb", [HALF, D], F32).ap()

    def shared(name, shape, dt):
        return nc.dram_tensor(name, shape, dt, addr_space="Shared").ap()

    uT = shared("uT", [SSMW, 1, SEQ], BF16)
    qT = shared("qT", [SSMW, 1, SEQ], BF16)
    kT = shared("kT", [SSMW, 1, SEQ], BF16)
    v = shared("v", [1, SEQ, SSMW], BF16)
    sg = shared("sg", [1, SEQ, SSMW], BF16)
    ysT = shared("ysT", [SSMW, 1, SEQ], BF16)
    retT = shared("retT", [SSMW, 1, SEQ], BF16)
    flags = shared("flags", [2, 16], I32)
    nblk = HALF // TB

    def tokT(t):
        return lambda p: t[:, 0, p * HALF:(p + 1) * HALF]

    def tokR(t):
        return lambda p: t[0, p * HALF:(p + 1) * HALF, :]

    def chT(t):
        return lambda p: t[p * 1024:(p + 1) * 1024, :, :]

    def chR(t):
        return lambda p: t[:, :, p * 1024:(p + 1) * 1024]

    def rows(t):
        return lambda blk, i: t[blk * TB + i * 128: blk * TB + (i + 1) * 128, :]

    with contextlib.ExitStack() as outer:
        kb = KB(nc, outer)
        kb.pair = True
        kb.consts(ident)
        for l in range(DEPTH):
            phase_mod(kb, c, W["w_ada"][l], W["b_ada"][l], modbuf[l], 0, 3 * D, pstride=3 * D)
        bar = 1
        pair_barrier(kb, flags, nonce, bar)
        for l in range(DEPTH):
            xin = x if l == 0 else xb
            phase_p1(kb, nblk, rows(xin), pos, modbuf[l], W["g_mix"][l], W["w_in"][l],
                     tokT(uT), tokT(qT), tokT(kT), tokR(v), tokR(sg), iota)
            bar += 1
            pair_barrier(kb, flags, nonce, bar)
            phase_s5(kb, 8, 1, WL["lam_re_l"][l], WL["lam_im_l"][l], WL["log_dt_l"][l], WL["b_re_l"][l], WL["b_im_l"][l],
                     WL["c_re_l"][l], WL["c_im_l"][l], WL["d_skip_l"][l], chT(uT), chT(ysT), iota512)
            phase_ret(kb, 4, 1, hidx, WL["g_ret_norm_l"][l], chT(qT), chT(kT), chR(v), chR(sg), chT(retT), iota512, iota)
            bar += 1
            pair_barrier(kb, flags, nonce, bar)
            phase_p3a(kb, nblk, rows(xin), rows(xa), modbuf[l], W["w_glu"][l], W["b_glu"][l], W["g_ssm_out"][l], W["w_out"][l],
                      tokT(ysT), tokT(retT))
            phase_p3b(kb, nblk, rows(xa), rows(xb), modbuf[l], W["g_mlp"][l], W["w_up"][l], W["w_down"][l])
        phase_fn(kb, HALF // 128, lambda i: xb[i * 128:(i + 1) * 128, :], lambda i: y[i * 128:(i + 1) * 128, :], W["g_final"])
        n_emitted = kb.S.n_emitted
    return nc, n_emitted


def kernel(**inputs):
    nc, _ = build_fused4()
    f32 = np.float32
    nonce = np.zeros((1, 16), np.int32)
    nonce[0, 0] = int(np.random.randint(1, 1 << 26))
    consts = {"ident": np.eye(128, dtype=f32), "iota": np.arange(128, dtype=f32), "iota512": np.arange(1, 513, dtype=f32), "nonce": nonce}
    loc_src = {"lam_re_l": ("lam_re", 64), "lam_im_l": ("lam_im", 64), "log_dt_l": ("log_dt", 64), "b_re_l": ("b_re", 64), "b_im_l": ("b_im", 64),
               "c_re_l": ("c_re", 64), "c_im_l": ("c_im", 64), "d_skip_l": ("d_skip", 1024), "g_ret_norm_l": ("g_ret_norm", 1024)}
    in_maps = []
    for pid in range(4):
        b, p = pid // 2, pid % 2
        m = {"x": np.ascontiguousarray(inputs["x"][b, p * HALF:(p + 1) * HALF], dtype=f32),
             "c": np.ascontiguousarray(inputs["c"][b], dtype=f32),
             "pos": np.ascontiguousarray(inputs["positions"][b, p * HALF:(p + 1) * HALF], dtype=np.int32),
             "hidx": np.arange(4 * p, 4 * p + 4, dtype=f32)}
        for n, s in _W4_SPECS:
            m[n] = np.ascontiguousarray(inputs[n], dtype=f32)
        for n, (src, w) in loc_src.items():
            m[n] = np.ascontiguousarray(np.asarray(inputs[src], dtype=f32)[:, p * w:(p + 1) * w])
        m.update(consts)
        in_maps.append(m)
    res = run_bass_kernel_spmd(nc, in_maps, core_ids=list(range(4)))
    out = np.empty((NB, SEQ, D), f32)
    for pid in range(4):
        b, p = pid // 2, pid % 2
        out[b, p * HALF:(p + 1) * HALF] = np.asarray(res.results[pid]["y"])
    return out
```

```python
import contextlib
import math
import numpy as np
import concourse.bass as bass
import concourse.mybir as mybir
from concourse.bass_utils import run_bass_kernel_spmd

F32 = mybir.dt.float32
BF16 = mybir.dt.bfloat16
I32 = mybir.dt.int32
AF = mybir.ActivationFunctionType
ALU = mybir.AluOpType

D = 4096
SEQ = 4096
NB = 2
DEPTH = 2
DFF = 16384
INW = 10240
SSMW = 2048
TB = 512
EPS = 1e-6
NCORES = 8
PI = math.pi

ENGS = ["pe", "act", "dve", "pool", "sp"]


class Op:
    __slots__ = ("eng", "fn", "deps", "idx", "signal", "count", "is_dma", "dsem", "dval", "inc")

    def __init__(self, eng, fn, is_dma):
        self.eng = eng
        self.fn = fn
        self.deps = set()
        self.idx = -1
        self.signal = False
        self.count = 0
        self.is_dma = is_dma
        self.dsem = -1
        self.dval = 0
        self.inc = 16


def _ap(x, p):
    return x(p) if callable(x) else x


class _Both:
    def __init__(self, a, b):
        self.a, self.b = a, b

    def then_inc(self, sem, v):
        self.a.then_inc(sem, v)
        self.b.then_inc(sem, v)


def _k(key):
    return tuple(key) if isinstance(key, (tuple, list)) else (key,)


class Sched:
    def __init__(self, n_dma_sems=48):
        self.ops = {e: [] for e in ENGS}
        self.res = {}
        self.n_dma_sems = n_dma_sems
        self.dma_last = [None] * n_dma_sems
        self.dma_rr = 0
        self.pending_dma = []

    def _related(self, key):
        d = self.res.setdefault(key[0], {})
        out = []
        for k, st in d.items():
            n = min(len(k), len(key))
            if k[:n] == key[:n]:
                out.append(st)
        return out

    def add(self, eng, fn, reads=(), writes=(), dma=False, inc=16, extra_deps=()):
        op = Op(eng, fn, dma)
        op.inc = inc
        deps = set(extra_deps)
        reads = [_k(x) for x in reads]
        writes = [_k(x) for x in writes]
        for key in reads:
            for st in self._related(key):
                if st["w"] is not None:
                    deps.add(st["w"])
        for key in writes:
            for st in self._related(key):
                if st["w"] is not None:
                    deps.add(st["w"])
                deps.update(st["r"].values())
                deps.update(st["rd"])
        for key in reads:
            d = self.res.setdefault(key[0], {})
            st = d.get(key)
            if st is None:
                st = {"w": None, "r": {}, "rd": []}
                d[key] = st
            if dma:
                st["rd"].append(op)
            else:
                st["r"][eng] = op
        for key in writes:
            d = self.res.setdefault(key[0], {})
            for k in list(d.keys()):
                if len(k) > len(key) and k[: len(key)] == key:
                    del d[k]
            d[key] = {"w": op, "r": {}, "rd": []}
        if dma:
            j = self.dma_rr
            self.dma_rr = (self.dma_rr + 1) % self.n_dma_sems
            prev = self.dma_last[j]
            if prev is not None:
                deps.add(prev)
            op.dsem = j
            op.dval = (prev.dval if prev is not None else 0) + inc
            self.dma_last[j] = op
            self.pending_dma.append(op)
        deps.discard(op)
        if eng == "pe" and not dma:
            deps = {x for x in deps if x.is_dma or x.eng != "pe"}
        op.deps = deps
        op.idx = len(self.ops[eng])
        self.ops[eng].append(op)
        for x in deps:
            if not x.is_dma:
                x.signal = True
        return op

    def barrier(self):
        deps = set(self.pending_dma)
        for e in ENGS:
            for op in reversed(self.ops[e]):
                if op.fn is not None and not op.is_dma:
                    deps.add(op)
                    break
        self.pending_dma = []
        self.res = {}
        for e in ENGS:
            self.add(e, None, extra_deps=deps)

    def setup(self, nc, stack):
        self.esem = {e: stack.enter_context(nc.semaphore("s_" + e)) for e in ENGS}
        self.dsem = [stack.enter_context(nc.semaphore("d_%d" % j)) for j in range(self.n_dma_sems)]
        self.cbase = {e: 0 for e in ENGS}
        self.waited = {e: {} for e in ENGS}
        self.n_emitted = 0

    def flush(self, nc):
        self.barrier()
        esem, dsem = self.esem, self.dsem
        for e in ENGS:
            c = self.cbase[e]
            for op in self.ops[e]:
                if op.signal and not op.is_dma:
                    c += 1
                    op.count = c
            self.cbase[e] = c
        sched = self
        ops = self.ops
        self.ops = {e: [] for e in ENGS}
        self.dyn = {}

        def run(e, engobj):
            waited = sched.waited[e]
            for op in ops[e]:
                for x in op.deps:
                    if x.is_dma:
                        key, val, sem = ("d", x.dsem), x.dval, dsem[x.dsem]
                    else:
                        key, val, sem = ("e", x.eng), x.count, esem[x.eng]
                    if waited.get(key, 0) >= val:
                        continue
                    waited[key] = val
                    engobj.wait_ge(sem, val)
                if op.fn is None:
                    continue
                ins = op.fn(engobj)
                sched.n_emitted += 1
                if op.is_dma:
                    ins.then_inc(dsem[op.dsem], op.inc)
                elif op.signal:
                    ins.then_inc(esem[e], 1)

        with nc.Block() as block:
            block.tensor(lambda t: run("pe", t))
            block.scalar(lambda s: run("act", s))
            block.vector(lambda v: run("dve", v))
            block.gpsimd(lambda g: run("pool", g))
            block.sync(lambda s: run("sp", s))


class KB:
    def __init__(self, nc, outer):
        self.nc = nc
        self.S = Sched()
        self.outer = outer
        self.ps = [outer.enter_context(nc.psum_tensor("ps%d" % i, [128, 512], F32)) for i in range(8)]
        self.ident_f = outer.enter_context(nc.sbuf_tensor("ident_f", [128, 128], F32))
        self.ident_b = outer.enter_context(nc.sbuf_tensor("ident_b", [128, 128], BF16))
        self.ones_b = outer.enter_context(nc.sbuf_tensor("ones_b", [128, 128], BF16))
        self.slab_rr = 0
        self.psg = 0
        self.uid = 0
        self.S.setup(nc, outer)
        self.pair = False
        self.rp = None

    def get_rp(self, e):
        if self.rp is None:
            self.rp = e.alloc_register("parity")
            e.reg_alu(self.rp, e.to_reg(e.partition_id()), 2, ALU.mod)
        return self.rp

    def pdma(self, e, mk):
        if not self.pair:
            return e.dma_start(**mk(0))
        rp = self.get_rp(e)
        with e.If_eq(rp, 0):
            a = e.dma_start(**mk(0))
        with e.Else():
            b = e.dma_start(**mk(1))
        return _Both(a, b)

    def consts(self, ident_dram):
        S = self.S
        S.add("sp", lambda e: e.dma_start(out=self.ident_f[:], in_=ident_dram), writes=["ident_f"], dma=True)
        S.add("pool", lambda e: e.dma_start(out=self.ident_b[:], in_=ident_dram), writes=["ident_b"], dma=True)
        S.add("dve", lambda e: e.memset(self.ones_b[:], 1.0), writes=["ones_b"])

    @contextlib.contextmanager
    def phase(self):
        st = contextlib.ExitStack()
        nc = self.nc
        self.uid += 1
        uid = self.uid

        def sb(name, shape, dt):
            return st.enter_context(nc.sbuf_tensor("%s_%d" % (name, uid), shape, dt))

        with st:
            yield sb
            self.S.flush(nc)

    def load_slab(self, slabs, w2d, k0, nk, c0, ncols=512):
        S = self.S
        slot = self.slab_rr % len(slabs)
        self.slab_rr += 1
        t = slabs[slot]
        key = ("slab", id(slabs), slot)
        step = 8
        for kk in range(0, nk, step):
            n = min(step, nk - kk)
            src = w2d[(k0 + kk) * 128:(k0 + kk + n) * 128, c0:c0 + ncols].rearrange("(k p) c -> p k c", p=128)
            S.add("pool", lambda e, kk=kk, n=n, src=src: e.dma_start(out=t[:, kk:kk + n, 0:ncols], in_=src),
                  writes=[key + (kk,)], dma=True)
        return t, key

    def psum_group(self):
        g = self.psg
        self.psg ^= 1
        return [g * 4 + j for j in range(4)]

    def gemm_fm(self, slabs, act, act_key, nkt, w2d, col0, ncols, evac, kslab=16):
        S = self.S
        for cs in range(col0, col0 + ncols, 512):
            nc_ = min(512, col0 + ncols - cs)
            nj = nc_ // 128
            banks = self.psum_group()
            for ks in range(0, nkt, kslab):
                nk = min(kslab, nkt - ks)
                t, key = self.load_slab(slabs, w2d, ks, nk, cs, nc_)
                for j in range(nj):
                    for k in range(nk):
                        kk = ks + k
                        S.add("pe", lambda e, j=j, k=k, kk=kk, t=t, b=banks[j]: e.matmul(
                            self.ps[b][:], t[:, k, j * 128:(j + 1) * 128], act[:, kk, :],
                            start=(kk == 0), stop=(kk == nkt - 1)),
                            reads=[key + ((k // 8) * 8,), act_key + (kk,)], writes=[("ps", banks[j])])
            for j in range(nj):
                evac(j, cs + j * 128, banks[j])

    def gemm_tm(self, slabs, act, act_key, nkt, w2d, col0, ncols, evac, ntt=4, kslab=16):
        S = self.S
        for cs in range(col0, col0 + ncols, 512):
            banks = self.psum_group()
            for ks in range(0, nkt, kslab):
                nk = min(kslab, nkt - ks)
                t, key = self.load_slab(slabs, w2d, ks, nk, cs, 512)
                for i in range(ntt):
                    for k in range(nk):
                        kk = ks + k
                        S.add("pe", lambda e, i=i, k=k, kk=kk, t=t, b=banks[i]: e.matmul(
                            self.ps[b][:], act[:, kk, i * 128:(i + 1) * 128], t[:, k, :],
                            start=(kk == 0), stop=(kk == nkt - 1)),
                            reads=[key + ((k // 8) * 8,), act_key + (kk,)], writes=[("ps", banks[i])])
            for i in range(ntt):
                evac(i, cs, banks[i])

    def normT(self, sb, x_rows, hT, hkey, gs, shift, gkey, tag):
        S = self.S
        xt = [sb("nx%s%d" % (tag, i), [128, D], F32) for i in range(2)]
        junk = sb("njunk" + tag, [128, D], BF16)
        st = sb("nst" + tag, [128, 8], F32)
        for i in range(4):
            x = xt[i % 2]
            xk = ("nx" + tag, i % 2)
            sk = ("nst" + tag, i % 2)
            c = (i % 2) * 4
            S.add("sp", lambda e, x=x, i=i: e.dma_start(out=x[:], in_=x_rows(i)), writes=[xk], dma=True)
            S.add("act", lambda e, x=x, c=c: e.activation(out=junk[:], in_=x[:], func=AF.Square, accum_out=st[:, c:c + 1]),
                  reads=[xk], writes=["njunk" + tag, sk])
            S.add("dve", lambda e, c=c: e.tensor_scalar(out=st[:, c + 1:c + 2], in0=st[:, c:c + 1], scalar1=1.0 / D, scalar2=EPS,
                                                      op0=ALU.mult, op1=ALU.add), reads=[sk], writes=[sk])
            S.add("act", lambda e, c=c: e.activation(out=st[:, c + 2:c + 3], in_=st[:, c + 1:c + 2], func=AF.Sqrt), reads=[sk], writes=[sk])
            S.add("dve", lambda e, c=c: e.reciprocal(out=st[:, c + 3:c + 4], in_=st[:, c + 2:c + 3]), reads=[sk], writes=[sk])
            S.add("dve", lambda e, x=x, c=c: e.tensor_scalar(out=x[:], in0=x[:], scalar1=st[:, c + 3:c + 4], scalar2=None, op0=ALU.mult),
                  reads=[xk, sk], writes=[xk])
            for t0 in range(0, 32, 4):
                bank = self.psum_group()[0]
                for q in range(4):
                    t = t0 + q
                    S.add("pe", lambda e, x=x, t=t, q=q, bank=bank: e.matmul(
                        self.ps[bank][:, q * 128:(q + 1) * 128], x[:, t * 128:(t + 1) * 128], self.ident_f[:],
                        start=True, stop=True), reads=[xk, "ident_f"], writes=[("ps", bank)])
                for q in range(4):
                    t = t0 + q
                    S.add("act", lambda e, t=t, q=q, bank=bank, i=i: e.activation(
                        out=hT[:, t, i * 128:(i + 1) * 128], in_=self.ps[bank][:, q * 128:(q + 1) * 128],
                        func=AF.Identity, scale=gs[:, t:t + 1], bias=shift[:, t:t + 1]),
                        reads=[("ps", bank), gkey], writes=[hkey + (t,)])

    def sin_tab(self, out, okey, ang, akeys, off, tmp, tmpi, tkey):
        S = self.S
        y, f = tmp[:, 0, :], tmp[:, 1, :]
        S.add("dve", lambda e: e.tensor_scalar(out=y, in0=ang, scalar1=1.0 / (2 * PI), scalar2=(off + PI) / (2 * PI),
                                               op0=ALU.mult, op1=ALU.add), reads=akeys, writes=[(tkey, 0)])
        S.add("dve", lambda e: e.tensor_copy(out=tmpi[:], in_=y), reads=[(tkey, 0)], writes=[(tkey, "i")])
        S.add("dve", lambda e: e.tensor_copy(out=f, in_=tmpi[:]), reads=[(tkey, "i")], writes=[(tkey, 1)])
        S.add("dve", lambda e: e.tensor_tensor(out=y, in0=y, in1=f, op=ALU.subtract), reads=[(tkey, 0), (tkey, 1)], writes=[(tkey, 0)])
        S.add("dve", lambda e: e.scalar_tensor_tensor(out=f, in0=y, scalar=0.0, in1=y, op0=ALU.is_lt, op1=ALU.add),
              reads=[(tkey, 0)], writes=[(tkey, 1)])
        S.add("dve", lambda e: e.tensor_scalar(out=f, in0=f, scalar1=2 * PI, scalar2=-PI, op0=ALU.mult, op1=ALU.add),
              reads=[(tkey, 1)], writes=[(tkey, 1)])
        S.add("act", lambda e: e.activation(out=out, in_=f, func=AF.Sin), reads=[(tkey, 1)], writes=[okey])

    def load_fm(self, dst, dkey, vec1d, ntile, col=0):
        src = vec1d.rearrange("(t p) -> p t", p=128)
        self.S.add("sp", lambda e: e.dma_start(out=dst[:, col:col + ntile], in_=src, allow_slow_non_contiguous=True),
                   writes=[dkey], dma=True)

    def load_bc(self, dst, dkey, vec1d, n):
        src = vec1d.partition_broadcast(128)
        self.S.add("sp", lambda e: e.dma_start(out=dst[:, 0:n], in_=src), writes=[dkey], dma=True)


def phase_mod(kb, c_vec, w_ada, b_ada, modout, col0, ncols, pstride=0):
    S = kb.S
    with kb.phase() as sb:
        cT = sb("cT", [128, 32], F32)
        slabs = [sb("mslab%d" % i, [128, 32, 512], F32) for i in range(2)]
        brow = [sb("brow%d" % i, [1, 512], F32) for i in range(2)]
        orow = [sb("orow%d" % i, [1, 512], F32) for i in range(2)]
        kb.load_fm(cT, ("cT",), c_vec, 32)
        S.add("act", lambda e: e.activation(out=cT[:], in_=cT[:], func=AF.Silu), reads=["cT"], writes=["cT"])
        for gi, cs in enumerate(range(col0, col0 + ncols, 512)):
            sl = slabs[gi % 2]
            skey = ("mslab", gi % 2)
            br, orr = brow[gi % 2], orow[gi % 2]
            for kk in range(0, 32, 8):
                S.add("sp", lambda e, sl=sl, kk=kk, cs=cs: kb.pdma(e, lambda p: dict(
                    out=sl[:, kk:kk + 8, :],
                    in_=w_ada[kk * 128:(kk + 8) * 128, cs + p * pstride:cs + p * pstride + 512].rearrange("(k p) c -> p k c", p=128))),
                    writes=[skey + (kk,)], dma=True)
            bank = kb.psum_group()[0]
            for k in range(32):
                S.add("pe", lambda e, sl=sl, k=k, bank=bank: e.matmul(kb.ps[bank][0:1, :], cT[:, k:k + 1], sl[:, k, :],
                                                                     start=(k == 0), stop=(k == 31)),
                      reads=["cT", skey + ((k // 8) * 8,)], writes=[("ps", bank)])
            S.add("sp", lambda e, cs=cs, br=br: kb.pdma(e, lambda p: dict(
                out=br[:], in_=b_ada[cs + p * pstride:cs + p * pstride + 512].rearrange("(a c) -> a c", a=1))), writes=[("brow", gi % 2)], dma=True)
            S.add("dve", lambda e, bank=bank, br=br, orr=orr: e.tensor_tensor(out=orr[:], in0=kb.ps[bank][0:1, :], in1=br[:], op=ALU.add),
                  reads=[("ps", bank), ("brow", gi % 2)], writes=[("orow", gi % 2)])
            S.add("sp", lambda e, cs=cs, orr=orr: kb.pdma(e, lambda p: dict(
                out=modout[cs + p * pstride:cs + p * pstride + 512].rearrange("(a c) -> a c", a=1), in_=orr[:])),
                reads=[("orow", gi % 2)], writes=["modout"], dma=True)


def phase_p1(kb, nblk, x_rows, pos, mod, g_mix, w_in, uT, qT, kT, v, sg, iota_p):
    S = kb.S
    with kb.phase() as sb:
        slabs = [sb("slab%d" % i, [128, 16, 512], BF16) for i in range(4)]
        hT = sb("hT", [128, 32, 512], BF16)
        gs = sb("gs", [128, 32], F32)
        sh = sb("sh", [128, 32], F32)
        gm = sb("gm", [128, 32], F32)
        invf = sb("invf", [128, 1], F32)
        posi = sb("posi", [128, 512], I32)
        posf = sb("posf", [128, 512], F32)
        tabs = sb("tabs", [128, 4, 512], F32)
        rtmp = sb("rtmp", [128, 2, 512], F32)
        rtmpi = sb("rtmpi", [128, 512], I32)
        rt = [sb("rt%d" % i, [128, 4, 512], F32) for i in range(2)]
        stage = [sb("stage%d" % i, [128, 4, 512], BF16) for i in range(2)]
        kb.load_fm(sh, ("sh",), mod[0:D], 32)
        kb.load_fm(gs, ("gs",), mod[D:2 * D], 32)
        kb.load_fm(gm, ("gm",), g_mix, 32)
        S.add("dve", lambda e: e.scalar_tensor_tensor(out=gs[:], in0=gs[:], scalar=1.0, in1=gm[:], op0=ALU.add, op1=ALU.mult),
              reads=["gs", "gm"], writes=["gs"])
        S.add("sp", lambda e: e.dma_start(out=invf[:], in_=iota_p.rearrange("(p a) -> p a", a=1)), writes=["invf"], dma=True)
        S.add("act", lambda e: e.activation(out=invf[:], in_=invf[:], func=AF.Exp, scale=-math.log(10000.0) / 128.0),
              reads=["invf"], writes=["invf"])
        cnt = [0]
        for blk in range(nblk):
            t0 = blk * TB
            S.add("sp", lambda e, t0=t0: e.dma_start(out=posi[:], in_=pos[t0:t0 + TB].partition_broadcast(128)), writes=["posi"], dma=True)
            S.add("dve", lambda e: e.tensor_copy(out=posf[:], in_=posi[:]), reads=["posi"], writes=["posf"])
            S.add("dve", lambda e: e.tensor_scalar(out=posf[:], in0=posf[:], scalar1=invf[:, 0:1], scalar2=None, op0=ALU.mult),
                  reads=["posf", "invf"], writes=["posf"])
            for ti, off in ((0, PI / 2), (1, 0.0)):
                kb.sin_tab(tabs[:, ti, :], ("tabs", ti), posf[:], ["posf"], off, rtmp, rtmpi, "rtmp")
                S.add("dve", lambda e, ti=ti: e.tensor_scalar(out=tabs[:, ti + 2, :], in0=tabs[:, ti, :], scalar1=1.0 / 16.0, scalar2=None, op0=ALU.mult),
                      reads=[("tabs", ti)], writes=[("tabs", ti + 2)])
            _normT_cached(kb, sb, lambda i, blk=blk: x_rows(blk, i), hT, ("hT",), gs, sh, "p1")

            def evac_u(j, col, bank, t0=t0):
                sg_ = stage[cnt[0] % 2]
                skey = ("stage", cnt[0] % 2)
                S.add("act", lambda e, j=j, bank=bank, sg_=sg_: e.activation(out=sg_[:, j, :], in_=kb.ps[bank][:], func=AF.Copy),
                      reads=[("ps", bank)], writes=[skey + (j,)])
                if j == 3:
                    c0 = col - 384
                    S.add("sp", lambda e, sg_=sg_, c0=c0: kb.pdma(e, lambda p: dict(
                        out=_ap(uT, p)[c0:c0 + 512, t0:t0 + TB].rearrange("(j p) t -> p j t", p=128), in_=sg_[:])), reads=[skey], dma=True)
                    cnt[0] += 1

            def mk_evac_rot(dstT, cbase, tc, ts, t0=t0):
                prev = {}

                def evac(j, col, bank):
                    sg_ = stage[cnt[0] % 2]
                    skey = ("stage", cnt[0] % 2)
                    if j % 2 == 0:
                        prev["b"] = bank
                        return
                    a, b = prev["b"], bank
                    r = rt[(j // 2) % 2]
                    rk = ("rt", (j // 2) % 2)
                    A, Bp = kb.ps[a], kb.ps[b]
                    S.add("dve", lambda e: e.tensor_tensor(out=r[:, 0, :], in0=A[:], in1=tabs[:, tc, :], op=ALU.mult),
                          reads=[("ps", a), ("tabs", tc)], writes=[rk + (0,)])
                    S.add("dve", lambda e: e.tensor_tensor(out=r[:, 1, :], in0=Bp[:], in1=tabs[:, ts, :], op=ALU.mult),
                          reads=[("ps", b), ("tabs", ts)], writes=[rk + (1,)])
                    S.add("dve", lambda e: e.tensor_tensor(out=r[:, 2, :], in0=A[:], in1=tabs[:, ts, :], op=ALU.mult),
                          reads=[("ps", a), ("tabs", ts)], writes=[rk + (2,)])
                    S.add("dve", lambda e: e.tensor_tensor(out=r[:, 3, :], in0=Bp[:], in1=tabs[:, tc, :], op=ALU.mult),
                          reads=[("ps", b), ("tabs", tc)], writes=[rk + (3,)])
                    S.add("dve", lambda e: e.tensor_tensor(out=sg_[:, j - 1, :], in0=r[:, 0, :], in1=r[:, 1, :], op=ALU.subtract),
                          reads=[rk + (0,), rk + (1,)], writes=[skey + (j - 1,)])
                    S.add("dve", lambda e: e.tensor_tensor(out=sg_[:, j, :], in0=r[:, 2, :], in1=r[:, 3, :], op=ALU.add),
                          reads=[rk + (2,), rk + (3,)], writes=[skey + (j,)])
                    if j == 3:
                        c0 = col - 384 - cbase
                        S.add("sp", lambda e, c0=c0: kb.pdma(e, lambda p: dict(
                            out=_ap(dstT, p)[c0:c0 + 512, t0:t0 + TB].rearrange("(j p) t -> p j t", p=128), in_=sg_[:])), reads=[skey], dma=True)
                        cnt[0] += 1
                return evac

            kb.gemm_fm(slabs, hT, ("hT",), 32, w_in, 0, 2048, evac_u)
            kb.gemm_fm(slabs, hT, ("hT",), 32, w_in, 2048, 2048, mk_evac_rot(qT, 2048, 0, 1))
            kb.gemm_fm(slabs, hT, ("hT",), 32, w_in, 4096, 2048, mk_evac_rot(kT, 4096, 2, 3))

            def mk_evac_tm(dst2d, cbase, func, t0=t0):
                def evac(i, cs, bank):
                    sg_ = stage[cnt[0] % 2]
                    skey = ("stage", cnt[0] % 2)
                    S.add("act", lambda e: e.activation(out=sg_[:, i, :], in_=kb.ps[bank][:], func=func),
                          reads=[("ps", bank)], writes=[skey + (i,)])
                    if i == 3:
                        c0 = cs - cbase
                        S.add("sp", lambda e, c0=c0: kb.pdma(e, lambda p: dict(
                            out=_ap(dst2d, p)[t0:t0 + TB, c0:c0 + 512].rearrange("(i p) c -> p i c", p=128), in_=sg_[:])), reads=[skey], dma=True)
                        cnt[0] += 1
                return evac

            kb.gemm_tm(slabs, hT, ("hT",), 32, w_in, 6144, 2048, mk_evac_tm(v, 6144, AF.Copy))
            kb.gemm_tm(slabs, hT, ("hT",), 32, w_in, 8192, 2048, mk_evac_tm(sg, 8192, AF.Silu))


def _normT_cached(kb, sb, x_rows, hT, hkey, gs, shift, tag):
    cache = kb.__dict__.setdefault("_ntc", {})
    key = (kb.uid, tag)
    if key not in cache:
        cache[key] = {
            "xt": [sb("nx%s%d" % (tag, i), [128, D], F32) for i in range(2)],
            "junk": sb("njunk" + tag, [128, D], BF16),
            "st": sb("nst" + tag, [128, 8], F32),
        }
    c = cache[key]

    def fake_sb(name, shape, dt):
        if name.startswith("nx"):
            return c["xt"][int(name[-1])]
        if name.startswith("njunk"):
            return c["junk"]
        return c["st"]

    kb.normT(fake_sb, x_rows, hT, hkey, gs, shift, ("gs",), tag)


def mod_side_gen(kb, sb, c_vec, w_ada, b_ada, modout, col0, ncols, pstride, bank=3):
    S = kb.S
    cT = sb("mcT", [128, 32], F32)
    slabs = [sb("mss%d" % i, [128, 8, 512], F32) for i in range(2)]
    brow = [sb("msb%d" % i, [1, 512], F32) for i in range(2)]
    orow = [sb("mso%d" % i, [1, 512], F32) for i in range(2)]
    kb.load_fm(cT, ("mcT",), c_vec, 32)
    S.add("act", lambda e: e.activation(out=cT[:], in_=cT[:], func=AF.Silu), reads=["mcT"], writes=["mcT"])
    yield
    parts = [(cs, kk) for cs in range(col0, col0 + ncols, 512) for kk in range(0, 32, 8)]

    def issue(i):
        cs, kk = parts[i]
        sl = slabs[i % 2]
        S.add("sp", lambda e: kb.pdma(e, lambda p: dict(
            out=sl[:], in_=w_ada[kk * 128:(kk + 8) * 128, cs + p * pstride:cs + p * pstride + 512].rearrange("(k p) c -> p k c", p=128))),
            writes=[("mss", i % 2)], dma=True)

    issue(0)
    yield
    for i, (cs, kk) in enumerate(parts):
        if i + 1 < len(parts):
            issue(i + 1)
        sl = slabs[i % 2]
        gi = (cs - col0) // 512
        for k in range(8):
            S.add("pe", lambda e, sl=sl, k=k, kk=kk: e.matmul(kb.ps[bank][0:1, :], cT[:, kk + k:kk + k + 1], sl[:, k, :],
                                                             start=(kk + k == 0), stop=(kk + k == 31)),
                  reads=["mcT", ("mss", i % 2)], writes=[("ps", bank)])
        if kk == 24:
            br, orr = brow[gi % 2], orow[gi % 2]
            S.add("sp", lambda e, cs=cs, br=br: kb.pdma(e, lambda p: dict(
                out=br[:], in_=b_ada[cs + p * pstride:cs + p * pstride + 512].rearrange("(a c) -> a c", a=1))), writes=[("msb", gi % 2)], dma=True)
            S.add("dve", lambda e, br=br, orr=orr: e.tensor_tensor(out=orr[:], in0=kb.ps[bank][0:1, :], in1=br[:], op=ALU.add),
                  reads=[("ps", bank), ("msb", gi % 2)], writes=[("mso", gi % 2)])
            S.add("sp", lambda e, cs=cs, orr=orr: kb.pdma(e, lambda p: dict(
                out=modout[cs + p * pstride:cs + p * pstride + 512].rearrange("(a c) -> a c", a=1), in_=orr[:])),
                reads=[("mso", gi % 2)], writes=["modout"], dma=True)
        yield


S5X = "dve"


def phase_s5(kb, NUT, NBL, lam_re, lam_im, log_dt, b_re, b_im, c_re, c_im, d_skip, uT_in, ysT_out, iota512, side=None):
    S = kb.S
    NJ = 4 * NUT
    NCH = SEQ // TB
    with kb.phase() as sb:
        sc = sb("s5sc", [128, 16, NJ], F32)
        dsk = sb("s5dsk", [128, NUT], F32)
        io = sb("s5io", [128, 512], F32)
        stmp = sb("s5stmp", [128, 2, 512], F32)
        stmpi = sb("s5stmpi", [128, 512], I32)
        nat = [sb("s5nat%d" % i, [128, 128], F32) for i in range(4)]
        mats = sb("s5mats", [128, 4, 4, 128], BF16)
        tab = sb("s5tab", [128, 4, 5, 512], F32)
        ang = sb("s5ang", [128, 512], F32)
        ut_sb = [sb("s5u%d" % i, [128, SEQ], BF16) for i in range(2)]
        ys_sb = [sb("s5ys%d" % i, [128, SEQ], BF16) for i in range(2)]
        xb = sb("s5xb", [128, 4, 2, SEQ // 2], BF16)
        carry = sb("s5carry", [128, 4, 2], F32)
        xf = sb("s5xf", [128, 2, 2, 512], F32)
        wt = sb("s5wt", [128, 2, 6, 512], F32)
        ytmp = sb("s5ytmp", [128, 2, 512], F32)
        LRE, LIM, LDT, DT, AR, TH, R_, C1, S1, NRE, NIM, DEN, FRE, FIM, T1, T2 = range(16)
        side_g = side(sb) if side is not None else None

        def side_step():
            nonlocal side_g
            if side_g is not None:
                try:
                    next(side_g)
                except StopIteration:
                    side_g = None

        def col(i):
            return sc[:, i, :]

        def dve(fn, r, w):
            S.add("dve", fn, reads=[("s5sc", x) for x in r], writes=[("s5sc", x) for x in w])

        S.add("sp", lambda e: e.dma_start(out=col(LRE), in_=lam_re.rearrange("(j a) p -> (a p) j", a=2), allow_slow_non_contiguous=True),
              writes=[("s5sc", LRE)], dma=True)
        S.add("sp", lambda e: e.dma_start(out=col(LIM), in_=lam_im.rearrange("(j a) p -> (a p) j", a=2), allow_slow_non_contiguous=True),
              writes=[("s5sc", LIM)], dma=True)
        for a in range(2):
            src = log_dt.rearrange("(j a) -> a j", a=2)[a].partition_broadcast(64)
            S.add("sp", lambda e, a=a, src=src: e.dma_start(out=sc[a * 64:(a + 1) * 64, LDT, :], in_=src, allow_slow_non_contiguous=True),
                  writes=[("s5sc", LDT, a)], dma=True)
        kb.load_fm(dsk, ("s5dsk",), d_skip, NUT)
        S.add("sp", lambda e: e.dma_start(out=io[:], in_=iota512.partition_broadcast(128)), writes=["s5io"], dma=True)
        S.add("act", lambda e: e.activation(out=col(DT), in_=col(LDT), func=AF.Exp), reads=[("s5sc", LDT)], writes=[("s5sc", DT)])
        dve(lambda e: e.tensor_tensor(out=col(AR), in0=col(LRE), in1=col(DT), op=ALU.mult), [LRE, DT], [AR])
        dve(lambda e: e.tensor_tensor(out=col(TH), in0=col(LIM), in1=col(DT), op=ALU.mult), [LIM, DT], [TH])
        S.add("act", lambda e: e.activation(out=col(R_), in_=col(AR), func=AF.Exp), reads=[("s5sc", AR)], writes=[("s5sc", R_)])
        kb.sin_tab(col(C1), ("s5sc", C1), col(TH), [("s5sc", TH)], PI / 2, stmp[:, :, 0:NJ], stmpi[:, 0:NJ], "s5stmp")
        kb.sin_tab(col(S1), ("s5sc", S1), col(TH), [("s5sc", TH)], 0.0, stmp[:, :, 0:NJ], stmpi[:, 0:NJ], "s5stmp")
        dve(lambda e: e.tensor_tensor(out=col(NRE), in0=col(R_), in1=col(C1), op=ALU.mult), [R_, C1], [NRE])
        dve(lambda e: e.tensor_scalar(out=col(NRE), in0=col(NRE), scalar1=-1.0, scalar2=None, op0=ALU.add), [NRE], [NRE])
        dve(lambda e: e.tensor_tensor(out=col(NIM), in0=col(R_), in1=col(S1), op=ALU.mult), [R_, S1], [NIM])
        dve(lambda e: e.tensor_tensor(out=col(DEN), in0=col(LRE), in1=col(LRE), op=ALU.mult), [LRE], [DEN])
        dve(lambda e: e.tensor_tensor(out=col(T1), in0=col(LIM), in1=col(LIM), op=ALU.mult), [LIM], [T1])
        dve(lambda e: e.tensor_tensor(out=col(DEN), in0=col(DEN), in1=col(T1), op=ALU.add), [DEN, T1], [DEN])
        dve(lambda e: e.reciprocal(out=col(DEN), in_=col(DEN)), [DEN], [DEN])
        dve(lambda e: e.tensor_tensor(out=col(T1), in0=col(NRE), in1=col(LRE), op=ALU.mult), [NRE, LRE], [T1])
        dve(lambda e: e.tensor_tensor(out=col(T2), in0=col(NIM), in1=col(LIM), op=ALU.mult), [NIM, LIM], [T2])
        dve(lambda e: e.tensor_tensor(out=col(T1), in0=col(T1), in1=col(T2), op=ALU.add), [T1, T2], [T1])
        dve(lambda e: e.tensor_tensor(out=col(FRE), in0=col(T1), in1=col(DEN), op=ALU.mult), [T1, DEN], [FRE])
        dve(lambda e: e.tensor_tensor(out=col(T1), in0=col(NIM), in1=col(LRE), op=ALU.mult), [NIM, LRE], [T1])
        dve(lambda e: e.tensor_tensor(out=col(T2), in0=col(NRE), in1=col(LIM), op=ALU.mult), [NRE, LIM], [T2])
        dve(lambda e: e.tensor_tensor(out=col(T1), in0=col(T1), in1=col(T2), op=ALU.subtract), [T1, T2], [T1])
        dve(lambda e: e.tensor_tensor(out=col(FIM), in0=col(T1), in1=col(DEN), op=ALU.mult), [T1, DEN], [FIM])

        seq_i = 0
        for ut in range(NUT):
            for jj in range(4):
                j = ut * 4 + jj
                srcs = [b_re, b_im, c_re, c_im]
                for mi in range(4):
                    n = nat[mi]
                    nk = ("s5nat", mi)
                    S.add("dve", lambda e, n=n: e.memset(n[:], 0.0), writes=[nk])
                    for a in range(2):
                        g = 2 * j + a
                        c0 = (jj * 2 + a) * 16
                        if mi < 2:
                            S.add("sp", lambda e, n=n, a=a, g=g, c0=c0, src=srcs[mi]: e.dma_start(out=n[a * 64:(a + 1) * 64, c0:c0 + 16], in_=src[g]),
                                  writes=[nk + (a,)], dma=True)
                        else:
                            S.add("sp", lambda e, n=n, a=a, g=g, c0=c0, src=srcs[mi]: e.dma_start(out=n[c0:c0 + 16, a * 64:(a + 1) * 64], in_=src[g]),
                                  writes=[nk + (a,)], dma=True)
                    bank = kb.psum_group()[0]
                    S.add("pe", lambda e, n=n, bank=bank: e.matmul(kb.ps[bank][:, 0:128], n[:], kb.ident_f[:], start=True, stop=True),
                          reads=[nk], writes=[("ps", bank)])
                    S.add("act", lambda e, bank=bank, jj=jj, mi=mi: e.activation(out=mats[:, jj, mi, :], in_=kb.ps[bank][:, 0:128], func=AF.Copy,
                                                                               scale=(-1.0 if mi == 3 else 1.0)),
                          reads=[("ps", bank)], writes=[("s5mats", jj, mi)])
                S.add("dve", lambda e, j=j: e.tensor_scalar(out=ang[:], in0=io[:], scalar1=sc[:, TH, j:j + 1], scalar2=None, op0=ALU.mult),
                      reads=["s5io", ("s5sc", TH)], writes=["s5ang"])
                kb.sin_tab(tab[:, jj, 2, :], ("s5tab", jj, 2), ang[:], ["s5ang"], PI / 2, stmp, stmpi, "s5stmp")
                kb.sin_tab(tab[:, jj, 3, :], ("s5tab", jj, 3), ang[:], ["s5ang"], 0.0, stmp, stmpi, "s5stmp")
                tk = ("s5tab", jj)
                S.add("dve", lambda e, jj=jj, j=j: e.tensor_scalar(out=tab[:, jj, 0, :], in0=tab[:, jj, 3, :], scalar1=sc[:, FIM, j:j + 1], scalar2=None, op0=ALU.mult),
                      reads=[tk + (3,), ("s5sc", FIM)], writes=[tk + (0,)])
                S.add("dve", lambda e, jj=jj, j=j: e.scalar_tensor_tensor(out=tab[:, jj, 0, :], in0=tab[:, jj, 2, :], scalar=sc[:, FRE, j:j + 1], in1=tab[:, jj, 0, :],
                                                                          op0=ALU.mult, op1=ALU.add), reads=[tk + (2,), tk + (0,), ("s5sc", FRE)], writes=[tk + (0,)])
                S.add("dve", lambda e, jj=jj, j=j: e.tensor_scalar(out=tab[:, jj, 1, :], in0=tab[:, jj, 3, :], scalar1=sc[:, FRE, j:j + 1], scalar2=None, op0=ALU.mult),
                      reads=[tk + (3,), ("s5sc", FRE)], writes=[tk + (1,)])
                S.add("dve", lambda e, jj=jj, j=j: e.scalar_tensor_tensor(out=tab[:, jj, 1, :], in0=tab[:, jj, 2, :], scalar=sc[:, FIM, j:j + 1], in1=tab[:, jj, 1, :],
                                                                          op0=ALU.mult, op1=ALU.subtract), reads=[tk + (2,), tk + (1,), ("s5sc", FIM)], writes=[tk + (1,)])
                S.add("dve", lambda e, jj=jj, j=j: e.tensor_scalar(out=tab[:, jj, 4, :], in0=io[:], scalar1=0.0, scalar2=sc[:, R_, j:j + 1], op0=ALU.mult, op1=ALU.add),
                      reads=["s5io", ("s5sc", R_)], writes=[tk + (4,)])
            for b in range(NBL):
                u = ut_sb[seq_i % 2]
                uk = ("s5u", seq_i % 2)
                ys = ys_sb[seq_i % 2]
                yk = ("s5ys", seq_i % 2)
                seq_i += 1
                S.add("sp", lambda e, u=u, ut=ut, b=b: kb.pdma(e, lambda p: dict(out=u[:], in_=_ap(uT_in, p)[ut * 128:(ut + 1) * 128, b, :])), writes=[uk], dma=True)
                for hf in range(2):
                    def chunk_ops(jj, c, par, hf=hf, u=u, uk=uk):
                        T = lambda i, jj=jj: tab[:, jj, i, :]
                        tk = ("s5tab", jj)
                        cs = slice(c * TB, (c + 1) * TB)
                        cl = slice((c % 4) * TB, (c % 4 + 1) * TB)
                        ba, bb = kb.psum_group()[0:2]
                        S.add("pe", lambda e, ba=ba, jj=jj, u=u, cs=cs: e.matmul(kb.ps[ba][:], mats[:, jj, 0, :], u[:, cs], start=True, stop=True),
                              reads=[("s5mats", jj, 0), uk], writes=[("ps", ba)])
                        yield
                        S.add("pe", lambda e, bb=bb, jj=jj, u=u, cs=cs: e.matmul(kb.ps[bb][:], mats[:, jj, 1, :], u[:, cs], start=True, stop=True),
                              reads=[("s5mats", jj, 1), uk], writes=[("ps", bb)])
                        yield
                        A, Bp = kb.ps[ba], kb.ps[bb]
                        W = lambda i, par=par: wt[:, par, i, :]
                        wk = ("s5wt", par)
                        S.add("dve", lambda e, A=A, W=W, T=T: e.tensor_tensor(out=W(0), in0=A[:], in1=T(0), op=ALU.mult), reads=[("ps", ba), tk + (0,)], writes=[wk + (0,)])
                        yield
                        S.add("dve", lambda e, Bp=Bp, W=W, T=T: e.tensor_tensor(out=W(1), in0=Bp[:], in1=T(1), op=ALU.mult), reads=[("ps", bb), tk + (1,)], writes=[wk + (1,)])
                        yield
                        S.add("dve", lambda e, W=W: e.tensor_tensor(out=W(4), in0=W(0), in1=W(1), op=ALU.subtract), reads=[wk + (0,), wk + (1,)], writes=[wk + (4,)])
                        yield
                        S.add("dve", lambda e, A=A, W=W, T=T: e.tensor_tensor(out=W(2), in0=A[:], in1=T(1), op=ALU.mult), reads=[("ps", ba), tk + (1,)], writes=[wk + (2,)])
                        yield
                        S.add("dve", lambda e, Bp=Bp, W=W, T=T: e.tensor_tensor(out=W(3), in0=Bp[:], in1=T(0), op=ALU.mult), reads=[("ps", bb), tk + (0,)], writes=[wk + (3,)])
                        yield
                        S.add("dve", lambda e, W=W: e.tensor_tensor(out=W(5), in0=W(2), in1=W(3), op=ALU.add), reads=[wk + (2,), wk + (3,)], writes=[wk + (5,)])
                        yield
                        xfk = ("s5xf", par)
                        pk = ("s5xf", 1 - par)
                        for ri in range(2):
                            init = 0.0 if c == 0 else carry[:, jj, ri:ri + 1]
                            S.add("dve", lambda e, W=W, T=T, ri=ri, init=init: e.tensor_tensor_scan(out=W(ri), data0=T(4), data1=W(4 + ri), initial=init,
                                                                                                   op0=ALU.mult, op1=ALU.add),
                                  reads=[tk + (4,), wk + (4 + ri,)] + ([] if c == 0 else [("s5carry", jj, ri)]), writes=[wk + (ri,)])
                            yield
                        S.add(S5X, lambda e, W=W, T=T: e.tensor_tensor(out=W(2), in0=W(0), in1=T(2), op=ALU.mult), reads=[wk + (0,), tk + (2,)], writes=[wk + (2,)])
                        yield
                        S.add(S5X, lambda e, W=W, T=T: e.tensor_tensor(out=W(3), in0=W(1), in1=T(3), op=ALU.mult), reads=[wk + (1,), tk + (3,)], writes=[wk + (3,)])
                        yield
                        S.add(S5X, lambda e, W=W, par=par: e.tensor_tensor(out=xf[:, par, 0, :], in0=W(2), in1=W(3), op=ALU.subtract),
                              reads=[wk + (2,), wk + (3,)], writes=[xfk + (0,)])
                        yield
                        S.add(S5X, lambda e, W=W, T=T: e.tensor_tensor(out=W(4), in0=W(0), in1=T(3), op=ALU.mult), reads=[wk + (0,), tk + (3,)], writes=[wk + (4,)])
                        yield
                        S.add(S5X, lambda e, W=W, T=T: e.tensor_tensor(out=W(5), in0=W(1), in1=T(2), op=ALU.mult), reads=[wk + (1,), tk + (2,)], writes=[wk + (5,)])
                        yield
                        S.add(S5X, lambda e, W=W, par=par: e.tensor_tensor(out=xf[:, par, 1, :], in0=W(4), in1=W(5), op=ALU.add),
                              reads=[wk + (4,), wk + (5,)], writes=[xfk + (1,)])
                        yield
                        for ri in range(2):
                            S.add("act", lambda e, par=par, ri=ri, jj=jj, cl=cl: e.activation(out=xb[:, jj, ri, cl], in_=xf[:, par, ri, :], func=AF.Copy),
                                  reads=[xfk + (ri,)], writes=[("s5xb", jj, ri, c % 4)])
                            yield
                            S.add("act", lambda e, par=par, ri=ri, jj=jj: e.activation(out=carry[:, jj, ri:ri + 1], in_=xf[:, par, ri, TB - 1:TB], func=AF.Copy),
                                  reads=[xfk + (ri,)], writes=[("s5carry", jj, ri)])
                            yield
                    for jp in range(2):
                        for c in range(hf * 4, hf * 4 + 4):
                            side_step()
                            gens = [chunk_ops(2 * jp + si, c, si) for si in range(2)]
                            while gens:
                                for g_ in list(gens):
                                    try:
                                        next(g_)
                                    except StopIteration:
                                        gens.remove(g_)
                    for c in range(hf * 4, hf * 4 + 4):
                        cs = slice(c * TB, (c + 1) * TB)
                        cl = slice((c % 4) * TB, (c % 4 + 1) * TB)
                        bank = kb.psum_group()[0]
                        n = 0
                        for jj in range(4):
                            for ri in range(2):
                                S.add("pe", lambda e, bank=bank, jj=jj, ri=ri, cl=cl, n=n: e.matmul(kb.ps[bank][:], mats[:, jj, 2 + ri, :], xb[:, jj, ri, cl],
                                                                                                   start=(n == 0), stop=(n == 7)),
                                      reads=[("s5mats", jj, 2 + ri), ("s5xb", jj, ri, c % 4)], writes=[("ps", bank)])
                                n += 1
                        par = c % 2
                        S.add("dve", lambda e, u=u, cs=cs, ut=ut, bank=bank, par=par: e.scalar_tensor_tensor(
                            out=ytmp[:, par, :], in0=u[:, cs], scalar=dsk[:, ut:ut + 1], in1=kb.ps[bank][:], op0=ALU.mult, op1=ALU.add),
                            reads=[uk, "s5dsk", ("ps", bank)], writes=[("s5ytmp", par)])
                        S.add("act", lambda e, ys=ys, cs=cs, par=par: e.activation(out=ys[:, cs], in_=ytmp[:, par, :], func=AF.Gelu),
                              reads=[("s5ytmp", par)], writes=[yk + (c,)])
                S.add("sp", lambda e, ys=ys, ut=ut, b=b: kb.pdma(e, lambda p: dict(out=_ap(ysT_out, p)[ut * 128:(ut + 1) * 128, b, :], in_=ys[:])), reads=[yk], dma=True)
        while side_g is not None:
            side_step()


def phase_ret(kb, NH, NBL, hidx, g_ret_norm, qT_in, kT_in, v_in, sg_in, retT_out, iota512, iota_p):
    S = kb.S
    HS = SEQ // 2
    NCK = HS // 128
    with kb.phase() as sb:
        hv = sb("rhv", [128, 8, NH], F32)
        io = sb("rio", [128, 512], F32)
        ip = sb("rip", [128, 1], F32)
        dif = sb("rdif", [128, 128], F32)
        msk = sb("rmsk", [128, 128], F32)
        maskT = sb("rmaskT", [128, NH, 128], F32)
        cdT = sb("rcdT", [128, NH, 2, 128], F32)
        sdc = sb("rsdc", [128, NH], F32)
        ckd = sb("rckd", [128, NH], F32)
        gnT = sb("rgnT", [128, NH * 256], F32)
        qs = [sb("rq%d" % i, [128, 2, HS], BF16) for i in range(2)]
        ks = [sb("rk%d" % i, [128, 2, HS], BF16) for i in range(2)]
        vs = [sb("rv%d" % i, [128, NCK, 256], BF16) for i in range(2)]
        gsb = [sb("rg%d" % i, [128, NCK, 256], BF16) for i in range(2)]
        osb = [sb("ro%d" % i, [128, 2, HS], BF16) for i in range(2)]
        St = sb("rS", [128, 512], F32)
        Sb_ = sb("rSb", [128, 512], BF16)
        sm = [sb("rsm%d" % i, [128, 128], BF16) for i in range(2)]
        qd = [sb("rqd%d" % i, [128, 2, 128], BF16) for i in range(2)]
        kd = [sb("rkd%d" % i, [128, 256], BF16) for i in range(2)]
        on = [sb("ron%d" % i, [128, 256], F32) for i in range(2)]
        ob = [sb("rob%d" % i, [128, 256], BF16) for i in range(2)]
        stt = sb("rstt", [128, 2, 16], F32)

        S.add("sp", lambda e: e.dma_start(out=hv[:, 0, :], in_=hidx.partition_broadcast(128)), writes=[("rhv", 0)], dma=True)
        S.add("sp", lambda e: e.dma_start(out=io[:], in_=iota512.partition_broadcast(128)), writes=["rio"], dma=True)
        S.add("sp", lambda e: e.dma_start(out=ip[:], in_=iota_p.rearrange("(p a) -> p a", a=1)), writes=["rip"], dma=True)
        S.add("sp", lambda e: e.dma_start(out=gnT[:], in_=g_ret_norm.partition_broadcast(128)), writes=["rgnT"], dma=True)
        X, LG, T = 1, 2, 3
        S.add("dve", lambda e: e.tensor_scalar(out=hv[:, X, :], in0=hv[:, 0, :], scalar1=-math.log(2.0), scalar2=-5.0 * math.log(2.0), op0=ALU.mult, op1=ALU.add),
              reads=[("rhv", 0)], writes=[("rhv", X)])
        S.add("act", lambda e: e.activation(out=hv[:, X, :], in_=hv[:, X, :], func=AF.Exp), reads=[("rhv", X)], writes=[("rhv", X)])
        S.add("dve", lambda e: e.tensor_scalar(out=hv[:, T, :], in0=hv[:, X, :], scalar1=0.25, scalar2=1.0 / 3.0, op0=ALU.mult, op1=ALU.add),
              reads=[("rhv", X)], writes=[("rhv", T)])
        S.add("dve", lambda e: e.tensor_tensor(out=hv[:, T, :], in0=hv[:, T, :], in1=hv[:, X, :], op=ALU.mult), reads=[("rhv", X), ("rhv", T)], writes=[("rhv", T)])
        S.add("dve", lambda e: e.tensor_scalar(out=hv[:, T, :], in0=hv[:, T, :], scalar1=0.5, scalar2=None, op0=ALU.add), reads=[("rhv", T)], writes=[("rhv", T)])
        S.add("dve", lambda e: e.tensor_tensor(out=hv[:, T, :], in0=hv[:, T, :], in1=hv[:, X, :], op=ALU.mult), reads=[("rhv", X), ("rhv", T)], writes=[("rhv", T)])
        S.add("dve", lambda e: e.tensor_scalar(out=hv[:, T, :], in0=hv[:, T, :], scalar1=1.0, scalar2=None, op0=ALU.add), reads=[("rhv", T)], writes=[("rhv", T)])
        S.add("dve", lambda e: e.tensor_tensor(out=hv[:, LG, :], in0=hv[:, T, :], in1=hv[:, X, :], op=ALU.mult), reads=[("rhv", X), ("rhv", T)], writes=[("rhv", LG)])
        S.add("dve", lambda e: e.tensor_scalar(out=hv[:, LG, :], in0=hv[:, LG, :], scalar1=-1.0, scalar2=None, op0=ALU.mult), reads=[("rhv", LG)], writes=[("rhv", LG)])
        S.add("dve", lambda e: e.tensor_scalar(out=dif[:], in0=io[:, 0:128], scalar1=ip[:, 0:1], scalar2=-1.0, op0=ALU.subtract, op1=ALU.add),
              reads=["rio", "rip"], writes=["rdif"])
        S.add("dve", lambda e: e.tensor_scalar(out=msk[:], in0=dif[:], scalar1=0.0, scalar2=None, op0=ALU.is_ge), reads=["rdif"], writes=["rmsk"])
        S.add("dve", lambda e: e.tensor_scalar(out=dif[:], in0=dif[:], scalar1=0.0, scalar2=None, op0=ALU.max), reads=["rdif"], writes=["rdif"])
        S.add("act", lambda e: e.activation(out=ckd[:], in_=hv[:, LG, :], func=AF.Exp, scale=128.0), reads=[("rhv", LG)], writes=["rckd"])
        S.add("dve", lambda e: e.tensor_scalar(out=ip[:], in0=ip[:], scalar1=-1.0, scalar2=127.0, op0=ALU.mult, op1=ALU.add), reads=["rip"], writes=["rip"])
        S.add("dve", lambda e: e.tensor_scalar(out=sdc[:], in0=hv[:, LG, :], scalar1=ip[:, 0:1], scalar2=None, op0=ALU.mult), reads=["rip", ("rhv", LG)], writes=["rsdc"])
        S.add("act", lambda e: e.activation(out=sdc[:], in_=sdc[:], func=AF.Exp), reads=["rsdc"], writes=["rsdc"])
        for hh in range(NH):
            S.add("act", lambda e, hh=hh: e.activation(out=maskT[:, hh, :], in_=dif[:], func=AF.Exp, scale=hv[:, LG, hh:hh + 1]),
                  reads=["rdif", ("rhv", LG)], writes=[("rmaskT", hh)])
            S.add("dve", lambda e, hh=hh: e.tensor_tensor(out=maskT[:, hh, :], in0=maskT[:, hh, :], in1=msk[:], op=ALU.mult),
                  reads=[("rmaskT", hh), "rmsk"], writes=[("rmaskT", hh)])
            for dh in range(2):
                S.add("act", lambda e, hh=hh, dh=dh: e.activation(out=cdT[:, hh, dh, :], in_=io[:, 0:128], func=AF.Exp, scale=hv[:, LG, hh:hh + 1]),
                      reads=["rio", ("rhv", LG)], writes=[("rcdT", hh, dh)])

        li = 0
        for hh in range(NH):
            for b in range(NBL):
                S.add("dve", lambda e: e.memset(St[:], 0.0), writes=["rS"])
                S.add("dve", lambda e: e.memset(Sb_[:], 0.0), writes=["rSb"])
                for hf in range(2):
                    t0 = hf * HS
                    q, k, v, g, o = qs[li % 2], ks[li % 2], vs[li % 2], gsb[li % 2], osb[li % 2]
                    lk = li % 2
                    li += 1
                    qsrc = lambda p, hh=hh, b=b, t0=t0: _ap(qT_in, p)[hh * 256:(hh + 1) * 256, b, t0:t0 + HS].rearrange("(h p) t -> p h t", p=128)
                    ksrc = lambda p, hh=hh, b=b, t0=t0: _ap(kT_in, p)[hh * 256:(hh + 1) * 256, b, t0:t0 + HS].rearrange("(h p) t -> p h t", p=128)
                    vsrc = lambda p, hh=hh, b=b, t0=t0: _ap(v_in, p)[b, t0:t0 + HS, hh * 256:(hh + 1) * 256].rearrange("(c p) e -> p c e", p=128)
                    gsrc = lambda p, hh=hh, b=b, t0=t0: _ap(sg_in, p)[b, t0:t0 + HS, hh * 256:(hh + 1) * 256].rearrange("(c p) e -> p c e", p=128)
                    S.add("sp", lambda e, q=q, qsrc=qsrc: kb.pdma(e, lambda p: dict(out=q[:], in_=qsrc(p))), writes=[("rq", lk)], dma=True)
                    S.add("sp", lambda e, k=k, ksrc=ksrc: kb.pdma(e, lambda p: dict(out=k[:], in_=ksrc(p))), writes=[("rk", lk)], dma=True)
                    S.add("sp", lambda e, v=v, vsrc=vsrc: kb.pdma(e, lambda p: dict(out=v[:], in_=vsrc(p))), writes=[("rv", lk)], dma=True)
                    S.add("sp", lambda e, g=g, gsrc=gsrc: kb.pdma(e, lambda p: dict(out=g[:], in_=gsrc(p))), writes=[("rg", lk)], dma=True)
                    for c in range(NCK):
                        cs = slice(c * 128, (c + 1) * 128)
                        p = c % 2
                        bs, bo, bk = p, 2 + p, 4 + p
                        bkv, bt = 6, 7
                        for dh in range(2):
                            S.add("pe", lambda e, dh=dh, k=k, q=q, cs=cs, bs=bs: e.matmul(kb.ps[bs][:, 0:128], k[:, dh, cs], q[:, dh, cs], start=(dh == 0), stop=(dh == 1)),
                                  reads=[("rk", lk), ("rq", lk)], writes=[("ps", bs)])
                        S.add("dve", lambda e, p=p, bs=bs, hh=hh: e.tensor_tensor(out=sm[p][:], in0=kb.ps[bs][:, 0:128], in1=maskT[:, hh, :], op=ALU.mult),
                              reads=[("ps", bs), ("rmaskT", hh)], writes=[("rsm", p)])
                        S.add("dve", lambda e, p=p, q=q, cs=cs, hh=hh: e.tensor_tensor(out=qd[p][:], in0=q[:, :, cs], in1=cdT[:, hh, :, :], op=ALU.mult),
                              reads=[("rq", lk), ("rcdT", hh)], writes=[("rqd", p)])
                        S.add("pe", lambda e, p=p, v=v, c=c, bo=bo: e.matmul(kb.ps[bo][:, 0:256], sm[p][:], v[:, c, :], start=True, stop=False),
                              reads=[("rsm", p), ("rv", lk)], writes=[("ps", bo)])
                        for dh in range(2):
                            S.add("pe", lambda e, p=p, dh=dh, bo=bo: e.matmul(kb.ps[bo][:, 0:256], qd[p][:, dh, :], Sb_[:, dh * 256:(dh + 1) * 256], start=False, stop=(dh == 1)),
                                  reads=[("rqd", p), "rSb"], writes=[("ps", bo)])
                        for dh in range(2):
                            S.add("pe", lambda e, dh=dh, k=k, cs=cs, bk=bk: e.matmul(kb.ps[bk][:, dh * 128:(dh + 1) * 128], k[:, dh, cs], kb.ident_b[:], start=True, stop=True),
                                  reads=[("rk", lk)], writes=[("ps", bk)])
                        S.add("act", lambda e, p=p, bk=bk, hh=hh: e.activation(out=kd[p][:], in_=kb.ps[bk][:, 0:256], func=AF.Copy, scale=sdc[:, hh:hh + 1]),
                              reads=[("ps", bk), "rsdc"], writes=[("rkd", p)])
                        for dh in range(2):
                            S.add("pe", lambda e, p=p, dh=dh, v=v, c=c: e.matmul(kb.ps[bkv][:, dh * 256:(dh + 1) * 256], kd[p][:, dh * 128:(dh + 1) * 128], v[:, c, :], start=True, stop=True),
                                  reads=[("rkd", p), ("rv", lk)], writes=[("ps", bkv)])
                        S.add("dve", lambda e, hh=hh: e.scalar_tensor_tensor(out=St[:], in0=St[:], scalar=ckd[:, hh:hh + 1], in1=kb.ps[bkv][:], op0=ALU.mult, op1=ALU.add),
                              reads=["rS", "rckd", ("ps", bkv)], writes=["rS"])
                        S.add("act", lambda e: e.activation(out=Sb_[:], in_=St[:], func=AF.Copy), reads=["rS"], writes=["rSb"])
                        sk = ("rstt", p)
                        S.add("dve", lambda e, p=p, bo=bo: e.bn_stats(out=stt[:, p, 0:6], in_=kb.ps[bo][:, 0:256]), reads=[("ps", bo)], writes=[sk])
                        S.add("dve", lambda e, p=p: e.bn_aggr(out=stt[:, p, 6:8], in_=stt[:, p, 0:6]), reads=[sk], writes=[sk])
                        S.add("dve", lambda e, p=p: e.tensor_scalar(out=stt[:, p, 8:9], in0=stt[:, p, 7:8], scalar1=EPS, scalar2=None, op0=ALU.add), reads=[sk], writes=[sk])
                        S.add("act", lambda e, p=p: e.activation(out=stt[:, p, 9:10], in_=stt[:, p, 8:9], func=AF.Sqrt), reads=[sk], writes=[sk])
                        S.add("dve", lambda e, p=p: e.reciprocal(out=stt[:, p, 10:11], in_=stt[:, p, 9:10]), reads=[sk], writes=[sk])
                        S.add("dve", lambda e, p=p: e.scalar_tensor_tensor(out=stt[:, p, 11:12], in0=stt[:, p, 6:7], scalar=-1.0, in1=stt[:, p, 10:11], op0=ALU.mult, op1=ALU.mult),
                              reads=[sk], writes=[sk])
                        S.add("act", lambda e, p=p, bo=bo: e.activation(out=on[p][:], in_=kb.ps[bo][:, 0:256], func=AF.Identity, scale=stt[:, p, 10:11], bias=stt[:, p, 11:12]),
                              reads=[("ps", bo), sk], writes=[("ron", p)])
                        S.add("dve", lambda e, p=p, hh=hh: e.tensor_tensor(out=on[p][:], in0=on[p][:], in1=gnT[:, hh * 256:(hh + 1) * 256], op=ALU.mult),
                              reads=[("ron", p), "rgnT"], writes=[("ron", p)])
                        S.add("dve", lambda e, p=p, g=g, c=c: e.tensor_tensor(out=ob[p][:], in0=on[p][:], in1=g[:, c, :], op=ALU.mult),
                              reads=[("ron", p), ("rg", lk)], writes=[("rob", p)])
                        for eh in range(2):
                            S.add("pe", lambda e, p=p, eh=eh: e.matmul(kb.ps[bt][:, eh * 128:(eh + 1) * 128], ob[p][:, eh * 128:(eh + 1) * 128], kb.ident_b[:], start=True, stop=True),
                                  reads=[("rob", p)], writes=[("ps", bt)])
                        S.add("act", lambda e, o=o, cs=cs: e.activation(out=o[:, :, cs], in_=kb.ps[bt][:, 0:256].rearrange("p (h n) -> p h n", h=2), func=AF.Copy),
                              reads=[("ps", bt)], writes=[("ro", lk, c)])
                    S.add("sp", lambda e, o=o, hh=hh, b=b, t0=t0: kb.pdma(e, lambda p: dict(
                        out=_ap(retT_out, p)[hh * 256:(hh + 1) * 256, b, t0:t0 + HS].rearrange("(h p) t -> p h t", p=128), in_=o[:])), reads=[("ro", lk)], dma=True)


def phase_p3a(kb, nblk, x_rows, xmix_rows, mod, w_glu, b_glu, g_ssm_out, w_out, ysT, retT):
    S = kb.S
    with kb.phase() as sb:
        slabs = [sb("slab%d" % i, [128, 16, 512], BF16) for i in range(4)]
        mixT = sb("mixT", [128, 32, 512], BF16)
        ysb = sb("ysb", [128, 16, 512], BF16)
        y2 = sb("y2", [128, 16, 512], F32)
        sig = [sb("sig%d" % i, [128, 512], F32) for i in range(2)]
        sq = [sb("sq%d" % i, [128, 512], F32) for i in range(2)]
        ssacc = sb("ssacc", [128, 512], F32)
        ssb = sb("ssb", [128, 512], BF16)
        rbc = sb("rbc", [128, 512], F32)
        bg = sb("bg", [128, 16], F32)
        gso = sb("gso", [128, 16], F32)
        gate = sb("gate", [128, D], F32)
        xp = [sb("xp%d" % i, [128, 512], F32) for i in range(4)]
        xo = [sb("xo%d" % i, [128, 512], F32) for i in range(4)]
        kb.load_fm(bg, ("bg",), b_glu, 16)
        kb.load_fm(gso, ("gso",), g_ssm_out, 16)
        kb.load_bc(gate, ("gate",), mod[2 * D:3 * D], D)
        cnt = [0]
        for blk in range(nblk):
            t0 = blk * TB
            S.add("sp", lambda e, t0=t0: kb.pdma(e, lambda p: dict(out=ysb[:], in_=_ap(ysT, p)[:, t0:t0 + TB].rearrange("(t p) n -> p t n", p=128))), writes=["ysb"], dma=True)
            S.add("sp", lambda e, t0=t0: kb.pdma(e, lambda p: dict(out=mixT[:, 16:32, :], in_=_ap(retT, p)[:, t0:t0 + TB].rearrange("(t p) n -> p t n", p=128))),
                  writes=[("mixT", t) for t in range(16, 32)], dma=True)

            def evac_glu(j, col, bank):
                m = col // 128
                p = m % 2
                S.add("act", lambda e: e.activation(out=sig[p][:], in_=kb.ps[bank][:], func=AF.Sigmoid, bias=bg[:, m:m + 1]),
                      reads=[("ps", bank), "bg"], writes=[("sig", p)])
                S.add("dve", lambda e: e.tensor_tensor(out=y2[:, m, :], in0=ysb[:, m, :], in1=sig[p][:], op=ALU.mult),
                      reads=[("ysb", m), ("sig", p)], writes=[("y2", m)])
                S.add("act", lambda e: e.activation(out=sq[p][:], in_=y2[:, m, :], func=AF.Square), reads=[("y2", m)], writes=[("sq", p)])
                if m == 0:
                    S.add("dve", lambda e: e.tensor_copy(out=ssacc[:], in_=sq[p][:]), reads=[("sq", p)], writes=["ssacc"])
                else:
                    S.add("dve", lambda e: e.tensor_tensor(out=ssacc[:], in0=ssacc[:], in1=sq[p][:], op=ALU.add), reads=[("sq", p), "ssacc"], writes=["ssacc"])

            kb.gemm_fm(slabs, ysb, ("ysb",), 16, w_glu, 0, 2048, evac_glu)
            bank = kb.psum_group()[0]
            S.add("act", lambda e: e.activation(out=ssb[:], in_=ssacc[:], func=AF.Copy), reads=["ssacc"], writes=["ssb"])
            S.add("pe", lambda e, bank=bank: e.matmul(kb.ps[bank][:], kb.ones_b[:], ssb[:], start=True, stop=True), reads=["ssb", "ones_b"], writes=[("ps", bank)])
            S.add("dve", lambda e, bank=bank: e.tensor_scalar(out=rbc[:], in0=kb.ps[bank][:], scalar1=1.0 / SSMW, scalar2=EPS, op0=ALU.mult, op1=ALU.add),
                  reads=[("ps", bank)], writes=["rbc"])
            S.add("act", lambda e: e.activation(out=rbc[:], in_=rbc[:], func=AF.Sqrt), reads=["rbc"], writes=["rbc"])
            S.add("dve", lambda e: e.reciprocal(out=rbc[:], in_=rbc[:]), reads=["rbc"], writes=["rbc"])
            for m in range(16):
                S.add("dve", lambda e, m=m: e.scalar_tensor_tensor(out=mixT[:, m, :], in0=y2[:, m, :], scalar=gso[:, m:m + 1], in1=rbc[:], op0=ALU.mult, op1=ALU.mult),
                      reads=[("y2", m), "gso", "rbc"], writes=[("mixT", m)])

            def evac_out(i, cs, bank, blk=blk):
                p = cnt[0] % 4
                cnt[0] += 1
                S.add("sp", lambda e: e.dma_start(out=xp[p][:], in_=x_rows(blk, i)[:, cs:cs + 512]), writes=[("xp", p)], dma=True)
                S.add("dve", lambda e: e.tensor_tensor(out=xo[p][:], in0=kb.ps[bank][:], in1=gate[:, cs:cs + 512], op=ALU.mult),
                      reads=[("ps", bank), "gate"], writes=[("xo", p)])
                S.add("dve", lambda e: e.tensor_tensor(out=xo[p][:], in0=xo[p][:], in1=xp[p][:], op=ALU.add), reads=[("xo", p), ("xp", p)], writes=[("xo", p)])
                S.add("sp", lambda e: e.dma_start(out=xmix_rows(blk, i)[:, cs:cs + 512], in_=xo[p][:]), reads=[("xo", p)], dma=True)

            kb.gemm_tm(slabs, mixT, ("mixT",), 32, w_out, 0, D, evac_out)


def phase_p3b(kb, nblk, xmix_rows, xout_rows, mod, g_mlp, w_up, w_down):
    S = kb.S
    FC = 2048
    with kb.phase() as sbo:
        hT = sbo("hT2", [128, 32, 512], BF16)
        gs = sbo("gs2", [128, 32], F32)
        sh = sbo("sh2", [128, 32], F32)
        gm = sbo("gm2", [128, 32], F32)
        kb.load_fm(sh, ("sh2",), mod[3 * D:4 * D], 32)
        kb.load_fm(gs, ("gs2",), mod[4 * D:5 * D], 32)
        kb.load_fm(gm, ("gm2",), g_mlp, 32)
        S.add("dve", lambda e: e.scalar_tensor_tensor(out=gs[:], in0=gs[:], scalar=1.0, in1=gm[:], op0=ALU.add, op1=ALU.mult),
              reads=["gs2", "gm2"], writes=["gs2"])
        for blk in range(nblk):
            with kb.phase() as sb:
                _normT_cached(kb, sb, lambda i, blk=blk: xmix_rows(blk, i), hT, ("hT2",), gs, sh, "p3b")
            with kb.phase() as sb:
                slabs = [sb("slab%d" % i, [128, 16, 512], BF16) for i in range(3)]
                aC = sb("aC", [128, 16, 512], BF16)
                acc = [sb("acc%d" % i, [128, D], F32) for i in range(4)]
                rl = [sb("rl%d" % i, [128, 512], F32) for i in range(2)]
                gp = [sb("gp%d" % i, [128, 512], F32) for i in range(2)]
                xp = [sb("xp%d" % i, [128, 512], F32) for i in range(2)]
                for fc in range(DFF // FC):
                    def evac_up(j, col, bank, fc=fc):
                        m = (col - fc * FC) // 128
                        p = m % 2
                        S.add("act", lambda e: e.activation(out=rl[p][:], in_=kb.ps[bank][:], func=AF.Relu), reads=[("ps", bank)], writes=[("rl", p)])
                        S.add("dve", lambda e: e.tensor_tensor(out=aC[:, m, :], in0=rl[p][:], in1=rl[p][:], op=ALU.mult), reads=[("rl", p)], writes=[("aC", m)])

                    kb.gemm_fm(slabs, hT, ("hT2",), 32, w_up, fc * FC, FC, evac_up)

                    def evac_dn(i, cs, bank, fc=fc):
                        if fc == 0:
                            S.add("dve", lambda e: e.tensor_copy(out=acc[i][:, cs:cs + 512], in_=kb.ps[bank][:]), reads=[("ps", bank)], writes=[("acc", i, cs)])
                        else:
                            S.add("dve", lambda e: e.tensor_tensor(out=acc[i][:, cs:cs + 512], in0=acc[i][:, cs:cs + 512], in1=kb.ps[bank][:], op=ALU.add),
                                  reads=[("ps", bank), ("acc", i, cs)], writes=[("acc", i, cs)])

                    kb.gemm_tm(slabs, aC, ("aC",), 16, w_down[fc * FC:(fc + 1) * FC, :], 0, D, evac_dn)
                n = 0
                for cs in range(0, D, 512):
                    g = gp[(cs // 512) % 2]
                    gk = ("gp", (cs // 512) % 2)
                    S.add("sp", lambda e, g=g, cs=cs: e.dma_start(out=g[:], in_=mod[5 * D + cs:5 * D + cs + 512].partition_broadcast(128)), writes=[gk], dma=True)
                    for i in range(4):
                        p = n % 2
                        n += 1
                        S.add("sp", lambda e, p=p, i=i, cs=cs, blk=blk: e.dma_start(out=xp[p][:], in_=xmix_rows(blk, i)[:, cs:cs + 512]), writes=[("xp", p)], dma=True)
                        S.add("dve", lambda e, i=i, cs=cs, g=g: e.tensor_tensor(out=acc[i][:, cs:cs + 512], in0=acc[i][:, cs:cs + 512], in1=g[:], op=ALU.mult),
                              reads=[("acc", i, cs), gk], writes=[("acc", i, cs)])
                        S.add("dve", lambda e, i=i, cs=cs, p=p: e.tensor_tensor(out=acc[i][:, cs:cs + 512], in0=acc[i][:, cs:cs + 512], in1=xp[p][:], op=ALU.add),
                              reads=[("acc", i, cs), ("xp", p)], writes=[("acc", i, cs)])
                for i in range(4):
                    S.add("sp", lambda e, i=i, blk=blk: e.dma_start(out=xout_rows(blk, i), in_=acc[i][:]), reads=[("acc", i)], dma=True)


def phase_fn(kb, ntiles, x_rows, y_rows, g_final):
    S = kb.S
    with kb.phase() as sb:
        gb = sb("fgb", [128, D], F32)
        xt = [sb("fx%d" % i, [128, D], F32) for i in range(2)]
        junk = sb("fjunk", [128, D], BF16)
        st = sb("fst", [128, 8], F32)
        kb.load_bc(gb, ("fgb",), g_final, D)
        for i in range(ntiles):
            x = xt[i % 2]
            xk = ("fx", i % 2)
            sk = ("fst", i % 2)
            c = (i % 2) * 4
            S.add("sp", lambda e, x=x, i=i: e.dma_start(out=x[:], in_=x_rows(i)), writes=[xk], dma=True)
            S.add("act", lambda e, x=x, c=c: e.activation(out=junk[:], in_=x[:], func=AF.Square, accum_out=st[:, c:c + 1]), reads=[xk], writes=["fjunk", sk])
            S.add("dve", lambda e, c=c: e.tensor_scalar(out=st[:, c + 1:c + 2], in0=st[:, c:c + 1], scalar1=1.0 / D, scalar2=EPS, op0=ALU.mult, op1=ALU.add), reads=[sk], writes=[sk])
            S.add("act", lambda e, c=c: e.activation(out=st[:, c + 2:c + 3], in_=st[:, c + 1:c + 2], func=AF.Sqrt), reads=[sk], writes=[sk])
            S.add("dve", lambda e, c=c: e.reciprocal(out=st[:, c + 3:c + 4], in_=st[:, c + 2:c + 3]), reads=[sk], writes=[sk])
            S.add("dve", lambda e, x=x, c=c: e.scalar_tensor_tensor(out=x[:], in0=x[:], scalar=st[:, c + 3:c + 4], in1=gb[:], op0=ALU.mult, op1=ALU.mult),
                  reads=[xk, sk, "fgb"], writes=[xk])
            S.add("sp", lambda e, x=x, i=i: e.dma_start(out=y_rows(i), in_=x[:]), reads=[xk], dma=True)


_W_SPECS = [
    ("w_ada", [DEPTH, D, 6 * D]), ("b_ada", [DEPTH, 6 * D]), ("g_mix", [DEPTH, D]), ("w_in", [DEPTH, D, INW]),
    ("lam_re", [DEPTH, 128, 64]), ("lam_im", [DEPTH, 128, 64]), ("log_dt", [DEPTH, 128]),
    ("b_re", [DEPTH, 128, 64, 16]), ("b_im", [DEPTH, 128, 64, 16]), ("c_re", [DEPTH, 128, 16, 64]), ("c_im", [DEPTH, 128, 16, 64]),
    ("d_skip", [DEPTH, SSMW]), ("w_glu", [DEPTH, SSMW, SSMW]), ("b_glu", [DEPTH, SSMW]), ("g_ssm_out", [DEPTH, SSMW]),
    ("g_ret_norm", [DEPTH, SSMW]), ("w_out", [DEPTH, D, D]), ("g_mlp", [DEPTH, D]), ("w_up", [DEPTH, D, DFF]),
    ("w_down", [DEPTH, DFF, D]), ("g_final", [D]),
]


def build_fused(debug=False):
    nc = bass.Bass("TRN2", target_bir_lowering=False)

    def din(name, shape, dt=F32):
        return nc.dram_tensor(name, shape, dt, kind="ExternalInput").ap()

    def dtmp(name, shape, dt, out=False):
        if out:
            return nc.dram_tensor(name, shape, dt, kind="ExternalOutput").ap()
        return nc.dram_tensor(name, shape, dt).ap()

    x = din("x", [SEQ, D])
    c = din("c", [D])
    pos = din("pos", [SEQ], I32)
    W = {n: din(n, s) for n, s in _W_SPECS}
    ident = din("ident", [128, 128])
    iota = din("iota", [128])
    iota512 = din("iota512", [512])
    hidx = din("hidx", [8])
    y = nc.dram_tensor("y", [SEQ, D], F32, kind="ExternalOutput").ap()
    modbuf = dtmp("modbuf", [DEPTH, 6 * D], F32, out=debug)
    xa = dtmp("xa", [SEQ, D], F32)
    xb = dtmp("xb", [SEQ, D], F32)
    xdbg = dtmp("xdbg", [TB, D], F32, out=True) if debug else None
    uT = dtmp("uT", [SSMW, 1, SEQ], BF16)
    qT = dtmp("qT", [SSMW, 1, SEQ], BF16)
    kT = dtmp("kT", [SSMW, 1, SEQ], BF16)
    v = dtmp("v", [1, SEQ, SSMW], BF16)
    sg = dtmp("sg", [1, SEQ, SSMW], BF16)
    ysT = dtmp("ysT", [SSMW, 1, SEQ], BF16)
    retT = dtmp("retT", [SSMW, 1, SEQ], BF16)
    nblk = SEQ // TB

    def rows(t):
        return lambda blk, i: t[blk * TB + i * 128: blk * TB + (i + 1) * 128, :]

    with contextlib.ExitStack() as outer:
        kb = KB(nc, outer)
        kb.consts(ident)
        for l in range(DEPTH):
            phase_mod(kb, c, W["w_ada"][l], W["b_ada"][l], modbuf[l], 0, 6 * D)
        for l in range(DEPTH):
            xin = x if l == 0 else xb
            phase_p1(kb, nblk, rows(xin), pos, modbuf[l], W["g_mix"][l], W["w_in"][l],
                     uT[:, 0, :], qT[:, 0, :], kT[:, 0, :], v[0], sg[0], iota)
            phase_s5(kb, SSMW // 128, 1, W["lam_re"][l], W["lam_im"][l], W["log_dt"][l], W["b_re"][l], W["b_im"][l],
                     W["c_re"][l], W["c_im"][l], W["d_skip"][l], uT, ysT, iota512)
            phase_ret(kb, 8, 1, hidx, W["g_ret_norm"][l], qT, kT, v, sg, retT, iota512, iota)
            phase_p3a(kb, nblk, rows(xin), rows(xa), modbuf[l], W["w_glu"][l], W["b_glu"][l], W["g_ssm_out"][l], W["w_out"][l],
                      ysT[:, 0, :], retT[:, 0, :])
            phase_p3b(kb, nblk, rows(xa), rows(xb), modbuf[l], W["g_mlp"][l], W["w_up"][l], W["w_down"][l])
            if debug and l == 0:
                with kb.phase() as sb:
                    t = sb("dbgt", [128, D], F32)
                    for i in range(4):
                        kb.S.add("sp", lambda e, i=i: e.dma_start(out=t[:], in_=xb[i * 128:(i + 1) * 128, :]), writes=["dbgt"], dma=True)
                        kb.S.add("sp", lambda e, i=i: e.dma_start(out=xdbg[i * 128:(i + 1) * 128, :], in_=t[:]), reads=["dbgt"], dma=True)
        phase_fn(kb, SEQ // 128, lambda i: xb[i * 128:(i + 1) * 128, :], lambda i: y[i * 128:(i + 1) * 128, :], W["g_final"])
        n_emitted = kb.S.n_emitted
    return nc, n_emitted


def kernel2(**inputs):
    nc, _ = build_fused(debug=False)
    f32 = np.float32
    consts = {
        "ident": np.eye(128, dtype=f32),
        "iota": np.arange(128, dtype=f32),
        "iota512": np.arange(1, 513, dtype=f32),
        "hidx": np.arange(8, dtype=f32),
    }
    in_maps = []
    for b in range(NB):
        m = {"x": np.ascontiguousarray(inputs["x"][b], dtype=f32),
             "c": np.ascontiguousarray(inputs["c"][b], dtype=f32),
             "pos": np.ascontiguousarray(inputs["positions"][b], dtype=np.int32)}
        for n, s in _W_SPECS:
            m[n] = np.ascontiguousarray(inputs[n], dtype=f32)
        m.update(consts)
        in_maps.append(m)
    res = run_bass_kernel_spmd(nc, in_maps, core_ids=list(range(NB)))
    return np.stack([np.asarray(res.results[b]["y"]) for b in range(NB)], axis=0).astype(f32)


HALF = SEQ // 2


def pair_barrier(kb, flags, nonce, k):
    def fn(sp):
        rp = kb.get_rp(sp)
        with sp.register("bn%d" % k) as rn, sp.register("br%d" % k) as r, sp.register("bc%d" % k) as r2, sp.register("bv%d" % k) as rv:
            sp.load(rn, nonce[0:1, 0:1])
            sp.reg_alu(rn, rn, 16, ALU.mult)
            sp.reg_alu(rv, rn, k, ALU.add)
            for p in range(2):
                g = sp.If_eq(rp, 0) if p == 0 else sp.Else()
                with g:
                    sp.store(flags[p:p + 1, 0:1], rv)
                    sp.reg_mov(r2, 1)
                    with sp.While(r2):
                        sp.nop(cycle_cnt=4096)
                        sp.load(r, flags[1 - p:2 - p, 0:1])
                        sp.reg_alu(r, r, rn, ALU.subtract)
                        sp.reg_alu(r2, r, k, ALU.is_lt)
                        sp.reg_alu(r, r, 15, ALU.is_gt)
                        sp.reg_alu(r2, r2, r, ALU.add)
        return sp.nop()
    kb.S.add("sp", fn)


_LOC_SPECS = [
    ("lam_re_l", [DEPTH, 64, 64]), ("lam_im_l", [DEPTH, 64, 64]), ("log_dt_l", [DEPTH, 64]),
    ("b_re_l", [DEPTH, 64, 64, 16]), ("b_im_l", [DEPTH, 64, 64, 16]), ("c_re_l", [DEPTH, 64, 16, 64]), ("c_im_l", [DEPTH, 64, 16, 64]),
    ("d_skip_l", [DEPTH, 1024]), ("g_ret_norm_l", [DEPTH, 1024]),
]
_W4_SPECS = [(n, s) for n, s in _W_SPECS if n not in ("lam_re", "lam_im", "log_dt", "b_re", "b_im", "c_re", "c_im", "d_skip", "g_ret_norm")]


def build_fused4():
    nc = bass.Bass("TRN2", target_bir_lowering=False)

    def din(name, shape, dt=F32):
        return nc.dram_tensor(name, shape, dt, kind="ExternalInput").ap()

    x = din("x", [HALF, D])
    c = din("c", [D])
    pos = din("pos", [HALF], I32)
    W = {n: din(n, s) for n, s in _W4_SPECS}
    WL = {n: din(n, s) for n, s in _LOC_SPECS}
    ident = din("ident", [128, 128])
    iota = din("iota", [128])
    iota512 = din("iota512", [512])
    hidx = din("hidx", [4])
    nonce = din("nonce", [1, 16], I32)
    y = nc.dram_tensor("y", [HALF, D], F32, kind="ExternalOutput").ap()
    modbuf = nc.dram_tensor("modbuf", [DEPTH, 6 * D], F32, addr_space="Shared").ap()
    xa = nc.dram_tensor("xa", [HALF, D], F32).ap()
    xb = nc.dram_tensor("xb", [HALF, D], F32).ap()

    def shared(name, shape, dt):
        return nc.dram_tensor(name, shape, dt, addr_space="Shared").ap()

    uT = shared("uT", [SSMW, 1, SEQ], BF16)
    qT = shared("qT", [SSMW, 1, SEQ], BF16)
    kT = shared("kT", [SSMW, 1, SEQ], BF16)
    v = shared("v", [1, SEQ, SSMW], BF16)
    sg = shared("sg", [1, SEQ, SSMW], BF16)
    ysT = shared("ysT", [SSMW, 1, SEQ], BF16)
    retT = shared("retT", [SSMW, 1, SEQ], BF16)
    flags = shared("flags", [2, 16], I32)
    nblk = HALF // TB

    def tokT(t):
        return lambda p: t[:, 0, p * HALF:(p + 1) * HALF]

    def tokR(t):
        return lambda p: t[0, p * HALF:(p + 1) * HALF, :]

    def chT(t):
        return lambda p: t[p * 1024:(p + 1) * 1024, :, :]

    def chR(t):
        return lambda p: t[:, :, p * 1024:(p + 1) * 1024]

    def rows(t):
        return lambda blk, i: t[blk * TB + i * 128: blk * TB + (i + 1) * 128, :]

    with contextlib.ExitStack() as outer:
        kb = KB(nc, outer)
        kb.pair = True
        kb.consts(ident)
        phase_mod(kb, c, W["w_ada"][0], W["b_ada"][0], modbuf[0], 0, 3 * D, pstride=3 * D)
        bar = 1
        pair_barrier(kb, flags, nonce, bar)
        for l in range(DEPTH):
            xin = x if l == 0 else xb
            phase_p1(kb, nblk, rows(xin), pos, modbuf[l], W["g_mix"][l], W["w_in"][l],
                     tokT(uT), tokT(qT), tokT(kT), tokR(v), tokR(sg), iota)
            bar += 1
            pair_barrier(kb, flags, nonce, bar)
            phase_s5(kb, 8, 1, WL["lam_re_l"][l], WL["lam_im_l"][l], WL["log_dt_l"][l], WL["b_re_l"][l], WL["b_im_l"][l],
                     WL["c_re_l"][l], WL["c_im_l"][l], WL["d_skip_l"][l], chT(uT), chT(ysT), iota512,
                     side=(lambda sb: mod_side_gen(kb, sb, c, W["w_ada"][1], W["b_ada"][1], modbuf[1], 0, 3 * D, 3 * D)) if l == 0 else None)
            phase_ret(kb, 4, 1, hidx, WL["g_ret_norm_l"][l], chT(qT), chT(kT), chR(v), chR(sg), chT(retT), iota512, iota)
            bar += 1
            pair_barrier(kb, flags, nonce, bar)
            phase_p3a(kb, nblk, rows(xin), rows(xa), modbuf[l], W["w_glu"][l], W["b_glu"][l], W["g_ssm_out"][l], W["w_out"][l],
                      tokT(ysT), tokT(retT))
            phase_p3b(kb, nblk, rows(xa), rows(xb), modbuf[l], W["g_mlp"][l], W["w_up"][l], W["w_down"][l])
        phase_fn(kb, HALF // 128, lambda i: xb[i * 128:(i + 1) * 128, :], lambda i: y[i * 128:(i + 1) * 128, :], W["g_final"])
        n_emitted = kb.S.n_emitted
    return nc, n_emitted


def kernel(**inputs):
    nc, _ = build_fused4()
    f32 = np.float32
    nonce = np.zeros((1, 16), np.int32)
    nonce[0, 0] = int(np.random.randint(1, 1 << 26))
    consts = {"ident": np.eye(128, dtype=f32), "iota": np.arange(128, dtype=f32), "iota512": np.arange(1, 513, dtype=f32), "nonce": nonce}
    loc_src = {"lam_re_l": ("lam_re", 64), "lam_im_l": ("lam_im", 64), "log_dt_l": ("log_dt", 64), "b_re_l": ("b_re", 64), "b_im_l": ("b_im", 64),
               "c_re_l": ("c_re", 64), "c_im_l": ("c_im", 64), "d_skip_l": ("d_skip", 1024), "g_ret_norm_l": ("g_ret_norm", 1024)}
    in_maps = []
    for pid in range(4):
        b, p = pid // 2, pid % 2
        m = {"x": np.ascontiguousarray(inputs["x"][b, p * HALF:(p + 1) * HALF], dtype=f32),
             "c": np.ascontiguousarray(inputs["c"][b], dtype=f32),
             "pos": np.ascontiguousarray(inputs["positions"][b, p * HALF:(p + 1) * HALF], dtype=np.int32),
             "hidx": np.arange(4 * p, 4 * p + 4, dtype=f32)}
        for n, s in _W4_SPECS:
            m[n] = np.ascontiguousarray(inputs[n], dtype=f32)
        for n, (src, w) in loc_src.items():
            m[n] = np.ascontiguousarray(np.asarray(inputs[src], dtype=f32)[:, p * w:(p + 1) * w])
        m.update(consts)
        in_maps.append(m)
    res = run_bass_kernel_spmd(nc, in_maps, core_ids=list(range(4)))
    out = np.empty((NB, SEQ, D), f32)
    for pid in range(4):
        b, p = pid // 2, pid % 2
        out[b, p * HALF:(p + 1) * HALF] = np.asarray(res.results[pid]["y"])
    return out
```
